# Optimizing a Trainium2 kernel written in Bass

```python
import jax, jax.numpy as jnp
from jax import lax
import numpy as np

D_MODEL = 1024
BATCH = 16
SEQ = 4096
DEPTH = 4

GRID_W = 64
CTX_LEN = 256
EPS = 1e-6

POOL_W = 256
POOL_GROUPS = 4
POOL_GC = POOL_W // POOL_GROUPS
POOL_WINDOWS = (2, 4, 8, 16)
MLA_HEADS = 8
MLA_NOPE = 64
MLA_ROPE = 32
MLA_V = 64
MLA_QK = MLA_NOPE + MLA_ROPE
Q_LORA = 256
KV_LORA = 128
Q_BLOCK = 128
ROPE_THETA = 10000.0
GLA_HEADS = 4
GLA_DK = 32
GLA_DV = 64
GLA_GATE_RANK = 16
GLA_GATE_NORM = 16.0
GLA_CHUNK = 64
MIX_W = POOL_W + MLA_HEADS * MLA_V + GLA_HEADS * GLA_DV
D_FF = 2816
CONV_W = 3

KEY_SIZES = (KV_LORA, MLA_ROPE, GLA_HEADS * GLA_DK, GLA_HEADS * GLA_DV, GLA_GATE_RANK, GLA_GATE_RANK)
QUERY_SIZES = (POOL_W, Q_LORA, GLA_HEADS * GLA_DK, GLA_HEADS * GLA_DV)
KEY_COLS = sum(KEY_SIZES)
IN_COLS = KEY_COLS + sum(QUERY_SIZES)

kernel_name = "hybrid_pool_mla_gla_diffusion_trunk"


def split_cols(z, sizes):
    out, o = [], 0
    for s in sizes:
        out.append(z[..., o:o + s])
        o += s
    return out


def rmsnorm(x, g):
    xf = x.astype(jnp.float32)
    y = xf * lax.rsqrt(jnp.mean(xf * xf, axis=-1, keepdims=True) + EPS)
    return y.astype(x.dtype) * g


def modulate(h, shift, scale):
    return h * (1.0 + scale) + shift


def rope_2d_tables(L):
    rows = L // GRID_W
    row = jnp.repeat(jnp.arange(rows), GRID_W).astype(jnp.float32)
    col = jnp.tile(jnp.arange(GRID_W), rows).astype(jnp.float32)
    half = MLA_ROPE // 2
    inv = ROPE_THETA ** (-jnp.arange(0, half, 2, dtype=jnp.float32) / half)
    ar = row[:, None] * inv
    ac = col[:, None] * inv
    ang = jnp.concatenate([ar, ar, ac, ac], axis=-1)
    return jnp.cos(ang), jnp.sin(ang)


def apply_rope(x, cos, sin):
    xf = x.astype(jnp.float32)
    xr = xf.reshape(xf.shape[:-1] + (2, 2, MLA_ROPE // 4))
    rot = jnp.stack([-xr[..., 1, :], xr[..., 0, :]], axis=-2).reshape(xf.shape)
    return (xf * cos + rot * sin).astype(x.dtype)


def multiscale_pool(u, w_pool, pool_scale):
    B, L, _ = u.shape
    P = jnp.pad(jnp.cumsum(u.astype(jnp.float32), axis=1), ((0, 0), (1, 0), (0, 0)))
    t = jnp.arange(L)
    means = []
    for gi, w in enumerate(POOL_WINDOWS):
        lo = jnp.clip(t - w // 2, 0, L)
        hi = jnp.clip(t - w // 2 + w, 0, L)
        Pg = P[..., gi * POOL_GC:(gi + 1) * POOL_GC]
        s = jnp.take(Pg, hi, axis=1) - jnp.take(Pg, lo, axis=1)
        means.append(s / (hi - lo).astype(jnp.float32)[None, :, None])
    pooled = jnp.stack(means, axis=2).astype(u.dtype)
    diff = pooled - u.reshape(B, L, POOL_GROUPS, POOL_GC)
    y = jnp.einsum('blgc,gcd->blgd', diff, w_pool).reshape(B, L, POOL_W)
    return y * pool_scale


def mla_softmax(qn, qr, kn, kr, v):
    s = jnp.einsum('bqhd,bkhd->bhqk', qn, kn) + jnp.einsum('bqhr,bkr->bhqk', qr, kr)
    p = jax.nn.softmax(s.astype(jnp.float32) * (MLA_QK ** -0.5), axis=-1)
    return jnp.einsum('bhqk,bkhd->bqhd', p.astype(v.dtype), v)


def mla_blockwise(qn, qr, kn, kr, v):
    B, L = qn.shape[:2]
    nb = L // Q_BLOCK
    to_blocks = lambda a: a.reshape((B, nb, Q_BLOCK) + a.shape[2:]).swapaxes(0, 1)
    out = lax.map(lambda qs: mla_softmax(qs[0], qs[1], kn, kr, v), (to_blocks(qn), to_blocks(qr)))
    return out.swapaxes(0, 1).reshape(B, L, MLA_HEADS, MLA_V)


def gla_gate(lr, w, b):
    B, L, _ = lr.shape
    g = jax.nn.log_sigmoid((lr @ w + b).astype(jnp.float32)) / GLA_GATE_NORM
    return g.reshape(B, L, GLA_HEADS, GLA_DK)


def gla_scan(k, v, g, s0, q=None):
    B, L, H, _ = k.shape
    n = L // GLA_CHUNK
    chunks = lambda a: a.astype(jnp.float32).reshape(B, n, GLA_CHUNK, H, a.shape[-1])
    kc, vc = chunks(k), chunks(v)
    b = jnp.cumsum(chunks(g), axis=2)
    b_last = b[:, :, -1:]
    kv = jnp.einsum('bnchd,bnchv->nbhdv', kc * jnp.exp(b_last - b), vc)
    decay = jnp.exp(b_last[:, :, 0]).swapaxes(0, 1)

    def step(s, inp):
        d, kvn = inp
        return d[..., None] * s + kvn, s

    s_fin, s_in = lax.scan(step, s0, (decay, kv))
    if q is None:
        return None, s_fin
    qc = chunks(q) * (GLA_DK ** -0.5)
    mid = b[:, :, GLA_CHUNK // 2:GLA_CHUNK // 2 + 1]
    tri = jnp.tril(jnp.ones((GLA_CHUNK, GLA_CHUNK), jnp.float32))
    att = jnp.einsum('bnihd,bnjhd->bnhij', qc * jnp.exp(b - mid), kc * jnp.exp(mid - b)) * tri
    o = (jnp.einsum('bnhij,bnjhv->bnihv', att, vc)
         + jnp.einsum('bnihd,nbhdv->bnihv', qc * jnp.exp(b), s_in))
    return o.reshape(B, L, H, GLA_DV).astype(v.dtype), s_fin


def token_mix(h_l, h_c, w_in, q_norm, w_uq, kv_norm, w_ukv, w_gk_f, b_gk_f, w_gk_b, b_gk_b,
              gla_norm, w_pool, pool_scale, cos, sin, ctx_out):
    B, L, _ = h_l.shape
    Lc = h_c.shape[1]
    flip = lambda a: a[:, ::-1]
    z_l = h_l @ w_in
    z_c = h_c @ (w_in if ctx_out else w_in[:, :KEY_COLS])
    ckv_l, kr_l, gk_l, gv_l, lrf_l, lrb_l, pool_l, cq_l, gq_l, og_l = split_cols(z_l, KEY_SIZES + QUERY_SIZES)
    ckv_c, kr_c, gk_c, gv_c, lrf_c, lrb_c = split_cols(z_c[..., :KEY_COLS], KEY_SIZES)

    def mla_kv(ckv):
        ukv = (rmsnorm(ckv, kv_norm) @ w_ukv).reshape(B, -1, MLA_HEADS, MLA_NOPE + MLA_V)
        return ukv[..., :MLA_NOPE], ukv[..., MLA_NOPE:]

    def mla_q(cq):
        uq = (rmsnorm(cq, q_norm) @ w_uq).reshape(B, -1, MLA_HEADS, MLA_QK)
        return uq[..., :MLA_NOPE], uq[..., MLA_NOPE:]

    kn_l, v_l = mla_kv(ckv_l)
    kn_c, v_c = mla_kv(ckv_c)
    kr_l = apply_rope(kr_l, cos, sin)
    kn_all = jnp.concatenate([kn_c, kn_l], axis=1)
    kr_all = jnp.concatenate([kr_c, kr_l], axis=1)
    v_all = jnp.concatenate([v_c, v_l], axis=1)
    qn_l, qr_l = mla_q(cq_l)
    qr_l = apply_rope(qr_l, cos[:, None, :], sin[:, None, :])
    att_l = mla_blockwise(qn_l, qr_l, kn_all, kr_all, v_all).reshape(B, L, MLA_HEADS * MLA_V)

    heads = lambda a, d: a.reshape(a.shape[0], a.shape[1], GLA_HEADS, d)
    s_zero = jnp.zeros((B, GLA_HEADS, GLA_DK, GLA_DV), jnp.float32)
    gq_c = heads(split_cols(z_c[..., KEY_COLS:], QUERY_SIZES)[2], GLA_DK) if ctx_out else None
    kc_, vc_ = heads(gk_c, GLA_DK), heads(gv_c, GLA_DV)
    o_cf, s_cf = gla_scan(kc_, vc_, gla_gate(lrf_c, w_gk_f, b_gk_f), s_zero, gq_c)
    o_cb, s_cb = gla_scan(flip(kc_), flip(vc_), flip(gla_gate(lrb_c, w_gk_b, b_gk_b)), s_zero,
                          None if gq_c is None else flip(gq_c))
    ql_, kl_, vl_ = heads(gq_l, GLA_DK), heads(gk_l, GLA_DK), heads(gv_l, GLA_DV)
    o_lf, _ = gla_scan(kl_, vl_, gla_gate(lrf_l, w_gk_f, b_gk_f), s_cf, ql_)
    o_lb, _ = gla_scan(flip(kl_), flip(vl_), flip(gla_gate(lrb_l, w_gk_b, b_gk_b)), s_cb, flip(ql_))

    def gla_out(o, og):
        y = rmsnorm(o, gla_norm) * jax.nn.silu(heads(og, GLA_DV))
        return y.reshape(y.shape[0], y.shape[1], GLA_HEADS * GLA_DV)

    gla_l = gla_out(o_lf + flip(o_lb), og_l)
    y_l = jnp.concatenate([multiscale_pool(pool_l, w_pool, pool_scale), att_l, gla_l], axis=-1)
    if not ctx_out:
        return y_l, None

    pool_c, cq_c, _, og_c = split_cols(z_c[..., KEY_COLS:], QUERY_SIZES)
    qn_c, qr_c = mla_q(cq_c)
    att_c = mla_softmax(qn_c, qr_c, kn_c, kr_c, v_c).reshape(B, Lc, MLA_HEADS * MLA_V)
    gla_c = gla_out(o_cf + flip(o_cb), og_c)
    y_c = jnp.concatenate([multiscale_pool(pool_c, w_pool, pool_scale), att_c, gla_c], axis=-1)
    return y_l, y_c


def conv_ffn(h, w_up, conv_w, conv_b, w_down):
    u, g = split_cols(h @ w_up, (D_FF, D_FF))
    gp = jnp.pad(g, ((0, 0), (1, 1), (0, 0)))
    g = gp[:, :-2] * conv_w[0] + gp[:, 1:-1] * conv_w[1] + gp[:, 2:] * conv_w[2] + conv_b
    return (jax.nn.silu(g) * u) @ w_down


def setup_inputs(seed: int = 0) -> dict:
    key = jax.random.key(seed)
    ks = jax.random.split(key, 32)
    nrm = lambda k, shape, scale: jax.random.normal(k, shape, jnp.float32) * scale
    gain = lambda k, shape: 1.0 + nrm(k, shape, 0.05)
    D, L = DEPTH, D_MODEL
    return {
        "x": nrm(ks[0], (BATCH, SEQ, D_MODEL), 1.0),
        "c": nrm(ks[1], (BATCH, D_MODEL), 1.0),
        "ctx": nrm(ks[2], (BATCH, CTX_LEN, D_MODEL), 1.0),
        "c_ctx": nrm(ks[3], (D_MODEL,), 1.0),
        "w_ada": nrm(ks[4], (D, D_MODEL, 6 * D_MODEL), 0.1 * D_MODEL ** -0.5),
        "b_ada": nrm(ks[5], (D, 6 * D_MODEL), 0.02),
        "norm1": gain(ks[6], (D, D_MODEL)),
        "norm2": gain(ks[7], (D, D_MODEL)),
        "w_in": nrm(ks[8], (D, D_MODEL, IN_COLS), D_MODEL ** -0.5),
        "q_norm": gain(ks[9], (D, Q_LORA)),
        "w_uq": nrm(ks[10], (D, Q_LORA, MLA_HEADS * MLA_QK), Q_LORA ** -0.5),
        "kv_norm": gain(ks[11], (D, KV_LORA)),
        "w_ukv": nrm(ks[12], (D, KV_LORA, MLA_HEADS * (MLA_NOPE + MLA_V)), KV_LORA ** -0.5),
        "w_gk_f": nrm(ks[13], (D, GLA_GATE_RANK, GLA_HEADS * GLA_DK), GLA_GATE_RANK ** -0.5),
        "b_gk_f": nrm(ks[14], (D, GLA_HEADS * GLA_DK), 0.1),
        "w_gk_b": nrm(ks[15], (D, GLA_GATE_RANK, GLA_HEADS * GLA_DK), GLA_GATE_RANK ** -0.5),
        "b_gk_b": nrm(ks[16], (D, GLA_HEADS * GLA_DK), 0.1),
        "gla_norm": gain(ks[17], (D, GLA_DV)),
        "w_pool": nrm(ks[18], (D, POOL_GROUPS, POOL_GC, POOL_GC), POOL_GC ** -0.5),
        "pool_scale": gain(ks[19], (D, POOL_W)),
        "w_o": nrm(ks[20], (D, MIX_W, D_MODEL), MIX_W ** -0.5),
        "w_up": nrm(ks[21], (D, D_MODEL, 2 * D_FF), D_MODEL ** -0.5),
        "conv_w": nrm(ks[22], (D, CONV_W, D_FF), CONV_W ** -0.5),
        "conv_b": nrm(ks[23], (D, D_FF), 0.02),
        "w_down": nrm(ks[24], (D, D_FF, D_MODEL), D_FF ** -0.5),
        "norm_f": gain(ks[25], (D_MODEL,)),
    }


def reference(x, c, ctx, c_ctx, w_ada, b_ada, norm1, norm2, w_in, q_norm, w_uq, kv_norm, w_ukv,
              w_gk_f, b_gk_f, w_gk_b, b_gk_b, gla_norm, w_pool, pool_scale, w_o, w_up, conv_w,
              conv_b, w_down, norm_f):
    B, L, Dm = x.shape
    cos, sin = rope_2d_tables(L)
    silu_c = jax.nn.silu(c)
    silu_cc = jax.nn.silu(c_ctx)
    xc = ctx
    for i in range(DEPTH):
        last = i == DEPTH - 1
        m_l = (silu_c @ w_ada[i] + b_ada[i]).reshape(B, 1, 6, Dm)
        n_c = 2 if last else 6
        m_c = (silu_cc @ w_ada[i][:, :n_c * Dm] + b_ada[i][:n_c * Dm]).reshape(n_c, Dm)
        h_l = modulate(rmsnorm(x, norm1[i]), m_l[:, :, 0], m_l[:, :, 1])
        h_c = modulate(rmsnorm(xc, norm1[i]), m_c[0], m_c[1])
        y_l, y_c = token_mix(h_l, h_c, w_in[i], q_norm[i], w_uq[i], kv_norm[i], w_ukv[i],
                             w_gk_f[i], b_gk_f[i], w_gk_b[i], b_gk_b[i], gla_norm[i],
                             w_pool[i], pool_scale[i], cos, sin, not last)
        x = x + m_l[:, :, 2] * (y_l @ w_o[i])
        x = x + m_l[:, :, 5] * conv_ffn(modulate(rmsnorm(x, norm2[i]), m_l[:, :, 3], m_l[:, :, 4]),
                                        w_up[i], conv_w[i], conv_b[i], w_down[i])
        if not last:
            xc = xc + m_c[2] * (y_c @ w_o[i])
            xc = xc + m_c[5] * conv_ffn(modulate(rmsnorm(xc, norm2[i]), m_c[3], m_c[4]),
                                        w_up[i], conv_w[i], conv_b[i], w_down[i])
    return rmsnorm(x, norm_f)
```

```python
import numpy as np
from contextlib import ExitStack
import concourse.bass as bass
import concourse.mybir as mybir
from concourse.bass_utils import run_bass_kernel_spmd

F32 = mybir.dt.float32
BF16 = mybir.dt.bfloat16
AF = mybir.ActivationFunctionType
ALU = mybir.AluOpType

ENGS = ("pe", "act", "dve", "pool", "sp")

NB = 2
D = 1024
L = 4096
LC = 256
NT = L + LC
DEPTH = 4
DFF = 2816
NFF = 22
EPS = 1e-6
INW = 1504


class Res:
    __slots__ = ("name", "w", "r")

    def __init__(self, name):
        self.name = name
        self.w = []
        self.r = []


class Sched:
    SAME_ENGINE_SYNC = False

    def __init__(self, nc, n_dma_sems=48):
        self.nc = nc
        self.q = {e: [] for e in ENGS}
        self.esem = {e: nc.alloc_semaphore("S_" + e) for e in ENGS if e != "sp"}
        self.dsems = [nc.alloc_semaphore("D%d" % i) for i in range(n_dma_sems)]
        self.dcount = [0] * n_dma_sems
        self.dq = {"sp": list(range(0, 28)), "pool": list(range(28, 44)), "act": list(range(44, 48))}
        self.dqi = {"sp": 0, "pool": 0, "act": 0}
        self.bar_sem = nc.alloc_semaphore("BAR")
        self.bar_count = 0
        self.bar_scratch = nc.dram_tensor("bar_scratch", [2, 64], F32).ap()
        self.all_res = []

    def res(self, name):
        r = Res(name)
        self.all_res.append(r)
        return r

    def _deps(self, eng, reads, writes, is_dma=False):
        deps = []
        for r in reads:
            deps.extend(r.w)
        for w in writes:
            if is_dma:
                deps.extend(t for t in w.w if t[0] != "d")
            else:
                deps.extend(w.w)
            deps.extend(w.r)
        cmax = {}
        dset = {}
        for d in deps:
            if d[0] == "c":
                if d[1] == eng and (eng == "pe" or not self.SAME_ENGINE_SYNC):
                    continue
                if cmax.get(d[1], -1) < d[2]:
                    cmax[d[1]] = d[2]
            else:
                if dset.get(d[1], 0) < d[2]:
                    dset[d[1]] = d[2]
        return [("c", e, i) for e, i in cmax.items()] + [("d", si, v) for si, v in dset.items()]

    def op(self, eng, fn, r=(), w=()):
        deps = self._deps(eng, r, w)
        idx = len(self.q[eng])
        import sys as _s
        fr = _s._getframe(2)
        self.q[eng].append({"fn": fn, "deps": deps, "kind": "c", "tag": "L%d" % fr.f_lineno})
        tok = ("c", eng, idx)
        for x in r:
            x.r.append(tok)
        for x in w:
            x.w = [tok]
            x.r = []
        return tok

    def dma(self, queue, out, in_, r=(), w=()):
        deps = self._deps(queue, r, w, is_dma=True)
        lst = self.dq[queue]
        si = lst[self.dqi[queue] % len(lst)]
        self.dqi[queue] += 1
        if self.dcount[si] > 0:
            deps = [d for d in deps if not (d[0] == "d" and d[1] == si)] + [("d", si, self.dcount[si])]
        self.dcount[si] += 16
        val = self.dcount[si]
        self.q[queue].append({"fn": (lambda e: e.dma_start(out=out, in_=in_)), "deps": deps, "kind": "d", "sem": si})
        tok = ("d", si, val)
        for x in r:
            x.r.append(tok)
        for x in w:
            x.w = [t for t in x.w if t[0] == "d"][-7:] + [tok]
            x.r = []
        return tok

    def barrier(self):
        deps = []
        for e in ENGS:
            if e == "sp":
                continue
            for j in range(len(self.q[e]) - 1, -1, -1):
                if self.q[e][j]["kind"] == "c":
                    deps.append(("c", e, j))
                    break
        for si in range(len(self.dsems)):
            if self.dcount[si] > 0:
                deps.append(("d", si, self.dcount[si]))
        self.bar_count += 16
        bc = self.bar_count
        sc = self.bar_scratch
        src = self.bar_src
        self.q["sp"].append({"fn": (lambda e: e.dma_start(out=sc[1:2, :], in_=src)), "deps": deps, "kind": "b"})
        for e in ENGS:
            self.q[e].append({"fn": None, "deps": [("b", bc)], "kind": "w"})
        for r in self.all_res:
            r.w = []
            r.r = []
        self.all_res = [r for r in self.all_res if getattr(r, "name", "").startswith("@")]

    def emit(self):
        nc = self.nc
        targets = {e: set() for e in ENGS}
        for e in ENGS:
            for o in self.q[e]:
                for d in o["deps"]:
                    if d[0] == "c":
                        targets[d[1]].add(d[2])
        rank = {}
        for e in ENGS:
            for i, idx in enumerate(sorted(targets[e])):
                rank[(e, idx)] = i + 1
        stats = {}

        def run(eng_name, e):
            waited = {}
            nw = 0
            if not hasattr(self, "_log"):
                self._log = {}
            lg = self._log.setdefault(eng_name, [])
            for idx, o in enumerate(self.q[eng_name]):
                for d in o["deps"]:
                    if d[0] == "c":
                        sem, val, key = self.esem[d[1]], rank[(d[1], d[2])], "c" + d[1]
                    elif d[0] == "d":
                        sem, val, key = self.dsems[d[1]], d[2], "d%d" % d[1]
                    else:
                        sem, val, key = self.bar_sem, d[1], "bar"
                    if waited.get(key, 0) >= val:
                        continue
                    waited[key] = val
                    e.wait_ge(sem, val)
                    lg.append("   wait %s >= %d" % (key, val))
                    nw += 1
                if o["fn"] is None:
                    continue
                ins = o["fn"](e)
                lg.append("%d %s %s inc=%s" % (idx, o["kind"], o.get("tag", ""), (eng_name, idx) in rank))
                if o["kind"] == "c":
                    if (eng_name, idx) in rank:
                        ins.then_inc(self.esem[eng_name], 1)
                elif o["kind"] == "d":
                    ins.then_inc(self.dsems[o["sem"]], 16)
                elif o["kind"] == "b":
                    ins.then_inc(self.bar_sem, 16)
            stats[eng_name] = (len(self.q[eng_name]), nw)
            import os
            if os.environ.get("DUMP"):
                with open(os.environ["DUMP"] + "_" + eng_name + ".txt", "w") as f:
                    for l_ in self._log[eng_name]:
                        f.write(l_ + "\n")

        with nc.Block() as block:
            @block.sync
            def _(e):
                run("sp", e)

            @block.tensor
            def _(e):
                run("pe", e)

            @block.scalar
            def _(e):
                run("act", e)

            @block.vector
            def _(e):
                run("dve", e)

            @block.gpsimd
            def _(e):
                run("pool", e)
        return stats


class K_:
    pass


def MM(K, out, lhsT, rhs, start, stop, r, w, skip=False):
    if skip:
        K.S.op("pe", lambda e: e.matmul(out, lhsT=lhsT, rhs=rhs, start=start, stop=stop, skip_group_check=True), r, w)
    else:
        K.S.op("pe", lambda e: e.matmul(out, lhsT=lhsT, rhs=rhs, start=start, stop=stop), r, w)


def ACT(K, out, in_, func, r, w, scale=None, bias=None):
    kw = {}
    if scale is not None:
        kw["scale"] = scale
    if bias is not None:
        kw["bias"] = bias
    K.S.op("act", lambda e: e.activation(out=out, in_=in_, func=func, **kw), r, w)


def TT(K, out, in0, in1, op, r, w, eng="dve"):
    K.S.op(eng, lambda e: e.tensor_tensor(out=out, in0=in0, in1=in1, op=op), r, w)


def TS(K, out, in0, s1, s2, op0, op1, r, w, eng="dve"):
    if op1 is None:
        K.S.op(eng, lambda e: e.tensor_scalar(out=out, in0=in0, scalar1=s1, scalar2=None, op0=op0), r, w)
    else:
        K.S.op(eng, lambda e: e.tensor_scalar(out=out, in0=in0, scalar1=s1, scalar2=s2, op0=op0, op1=op1), r, w)


def STT(K, out, in0, scalar, in1, op0, op1, r, w):
    K.S.op("dve", lambda e: e.scalar_tensor_tensor(out=out, in0=in0, scalar=scalar, in1=in1, op0=op0, op1=op1), r, w)


def CP(K, out, in_, r, w, eng="dve"):
    if eng == "act":
        K.S.op("act", lambda e: e.copy(out=out, in_=in_), r, w)
    else:
        K.S.op(eng, lambda e: e.tensor_copy(out=out, in_=in_), r, w)


def MSET(K, ap, val, w, eng="dve"):
    K.S.op(eng, lambda e: e.memset(ap, val), (), w)


def RECIP(K, out, in_, r, w):
    K.S.op("dve", lambda e: e.reciprocal(out=out, in_=in_), r, w)


class Phase:
    def __init__(self, K, name):
        self.K = K
        self.name = name
        self.st = ExitStack()
        self.n = 0

    def __enter__(self):
        self.st.__enter__()
        return self

    def __exit__(self, *a):
        self.K.S.barrier()
        return self.st.__exit__(*a)

    def sb(self, name, shape, dt):
        self.n += 1
        nm = "%s_%s_%d" % (self.name, name, self.n)
        t = self.st.enter_context(self.K.nc.sbuf_tensor(nm, shape, dt))
        return t, self.K.S.res(nm)

    def sbn(self, name, shape, dt, n):
        return [self.sb(name + str(i), shape, dt) for i in range(n)]


class Rot:
    def __init__(self, items):
        self.items = items
        self.i = 0

    def next(self):
        x = self.items[self.i % len(self.items)]
        self.i += 1
        return x


def load_w(K, P, dst, rdst, src, rows, cols, three_d=True, eng="pool"):
    if not hasattr(P, "_stg"):
        P._stg = Rot(P.sbn("stg", [128, 1024], F32, 3))
    nk = max(1, rows // 128)
    for kc in range(nk):
        rp = min(128, rows)
        for c0 in range(0, cols, 1024):
            wd = min(1024, cols - c0)
            stg, rstg = P._stg.next()
            K.S.dma("sp", stg[0:rp, 0:wd], src[kc * 128:kc * 128 + rp, c0:c0 + wd], w=[rstg])
            if three_d:
                o = dst[0:rp, kc, c0:c0 + wd]
            else:
                o = dst[0:rp, c0:c0 + wd]
            CP(K, o, stg[0:rp, 0:wd], [rstg], [rdst], eng=eng)


def tiles_of_batch():
    t = [(0, LC, True, 0)]
    for i in range(L // 512):
        t.append((LC + i * 512, 512, False, i * 512))
    return t


def rstd_from_psum(K, out, ps_ap, n, r, w):
    ACT(K, out, ps_ap, AF.Ln, r, w, scale=1.0 / n, bias=K.eps_col[:, 0:1])
    ACT(K, out, out, AF.Exp, w, w, scale=-0.5)


def phase_mod(K):
    nc, S, I = K.nc, K.S, K.I
    with Phase(K, "mod") as P:
        cT, rcT = P.sb("cT", [128, 8, 3], F32)
        sc, rsc = P.sb("sc", [128, 8, 3], F32)
        bad, rbad = P.sb("bad", [128, DEPTH * 48], F32)
        S.dma("sp", cT[:], I["cT"], w=[rcT])
        S.dma("sp", bad[:], I["b_adaT"], w=[rbad])
        ACT(K, sc[:], cT[:], AF.Silu, [rcT], [rsc])
        wb = Rot(P.sbn("wada", [128, 8, 512], F32, 2))
        psr = Rot(list(zip(K.PS, K.rPS)))
        for l in range(DEPTH):
            for cb in range(12):
                W, rW = wb.next()
                src = I["w_ada"][l].rearrange("(k p) c -> p k c", p=128)[:, :, cb * 512:(cb + 1) * 512]
                S.dma("sp", W[:], src, w=[rW])
                ps, rps = psr.next()
                for f in range(4):
                    for k in range(8):
                        MM(K, ps[:, f * 4:f * 4 + 3], W[:, k, f * 128:(f + 1) * 128], sc[:, k, :], k == 0, k == 7,
                           [rW, rsc], [rps])
                for f in range(4):
                    cidx = cb * 4 + f
                    m, kk = cidx // 8, cidx % 8
                    TS(K, K.MOD[:, l, m, kk, :], ps[:, f * 4:f * 4 + 3], bad[:, l * 48 + cidx:l * 48 + cidx + 1], None,
                       ALU.add, None, [rps, rbad], [K.rMOD])
        n1, rn1 = P.sb("n1", [128, DEPTH * 8], F32)
        n2, rn2 = P.sb("n2", [128, DEPTH * 8], F32)
        S.dma("sp", n1[:], I["norm1T"], w=[rn1])
        S.dma("sp", n2[:], I["norm2T"], w=[rn2])
        for l in range(DEPTH):
            for k in range(8):
                for (m, nn, rnn) in ((1, n1, rn1), (4, n2, rn2)):
                    TS(K, K.MOD[:, l, m, k, :], K.MOD[:, l, m, k, :], 1.0, nn[:, l * 8 + k:l * 8 + k + 1], ALU.add, ALU.mult,
                       [K.rMOD, rnn], [K.rMOD])


def phase_A(K, l):
    nc, S, I, SC = K.nc, K.S, K.I, K.SC
    with Phase(K, "A%d" % l) as P:
        Win, rWin = P.sb("Win", [128, 8, INW], BF16)
        Wuq, rWuq = P.sb("Wuq", [128, 2, 1024], BF16)
        Wukv, rWukv = P.sb("Wukv", [128, 1024], BF16)
        load_w(K, P, Win, rWin, I["w_in_ext"][l], 1024, INW)
        load_w(K, P, Wuq, rWuq, I["w_uq_ext"][l], 256, 1024)
        load_w(K, P, Wukv, rWukv, I["w_ukv_r"][l], 128, 1024, three_d=False)
        COS, rCOS = P.sb("cos", [128, L], F32)
        SIN, rSIN = P.sb("sin", [128, L], F32)
        S.dma("sp", COS[:], I["cos_t"], w=[rCOS])
        S.dma("sp", SIN[:], I["sin_t"], w=[rSIN])
        qn, rqn = P.sb("qn", [128, DEPTH * 2], F32)
        kvn, rkvn = P.sb("kvn", [128, DEPTH], F32)
        S.dma("sp", qn[:], I["q_normT"], w=[rqn])
        S.dma("sp", kvn[:], I["kv_normT"], w=[rkvn])
        Xb = Rot(P.sbn("X", [128, 8, 512], F32, 2))
        SQ, rSQ = P.sb("SQ", [128, 8, 512], BF16)
        RS, rRS = P.sb("RS", [128, 512], F32)
        TMP, rTMP = P.sb("TMP", [128, 512], F32)
        H, rH = P.sb("H", [128, 8, 512], BF16)
        SQ2, rSQ2 = P.sb("SQ2", [128, 2, 512], BF16)
        RS2, rRS2 = P.sb("RS2", [128, 512], F32)
        CKN, rCKN = P.sb("CKN", [128, 512], BF16)
        CQN, rCQN = P.sb("CQN", [128, 2, 512], BF16)
        VT, rVT = P.sb("VT", [128, 4, 8, 128], BF16)
        MSET(K, VT[:], 1.0, [rVT])
        GKVt, rGKVt = P.sb("GKVt", [128, 4, 384], BF16)
        T1, rT1 = P.sb("T1", [128, 512], F32)
        T2, rT2 = P.sb("T2", [128, 512], F32)
        ob = Rot(P.sbn("OB", [128, 512], BF16, 6))
        of = Rot(P.sbn("OF", [128, 512], F32, 3))
        psr = Rot(list(zip(K.PS, K.rPS)))
        src_x = K.x_src(l)
        for b in range(NB):
            xv = src_x[b].rearrange("(k p) t -> p k t", p=128)
            for (u0, T, isc, pos0) in tiles_of_batch():
                j = 2 if isc else b
                nblk = T // 128
                X, rX = Xb.next()
                S.dma("sp", X[:, :, 0:T], xv[:, :, u0:u0 + T], w=[rX])
                ACT(K, SQ[:, :, 0:T], X[:, :, 0:T], AF.Square, [rX], [rSQ])
                ps, rps = psr.next()
                for k in range(8):
                    MM(K, ps[:, 0:T], K.ONESB[:, :], SQ[:, k, 0:T], k == 0, k == 7, [rSQ, K.rONESB], [rps])
                rstd_from_psum(K, RS[:, 0:T], ps[:, 0:T], D, [rps], [rRS])
                for k in range(8):
                    STT(K, TMP[:, 0:T], X[:, k, 0:T], K.MOD[:, l, 1, k, j:j + 1], RS[:, 0:T], ALU.mult, ALU.mult,
                        [rX, rRS, K.rMOD], [rTMP])
                    ACT(K, H[:, k, 0:T], TMP[:, 0:T], AF.Identity, [rTMP, K.rMOD], [rH], bias=K.MOD[:, l, 0, k, j:j + 1])

                def proj(c0, m, ps_ap, rps_):
                    for k in range(8):
                        MM(K, ps_ap, Win[:, k, c0:c0 + m], H[:, k, 0:T], k == 0, k == 7, [rWin, rH], [rps_])

                ps, rps = psr.next()
                proj(0, 128, ps[:, 0:T], rps)
                ACT(K, SQ2[:, 0, 0:T], ps[:, 0:T], AF.Square, [rps], [rSQ2])
                ps2, rps2 = psr.next()
                MM(K, ps2[:, 0:T], K.ONESB[:, :], SQ2[:, 0, 0:T], True, True, [rSQ2, K.rONESB], [rps2])
                rstd_from_psum(K, RS2[:, 0:T], ps2[:, 0:T], 128, [rps2], [rRS2])
                STT(K, CKN[:, 0:T], ps[:, 0:T], kvn[:, l:l + 1], RS2[:, 0:T], ALU.mult, ALU.mult, [rps, rkvn, rRS2], [rCKN])
                for hp in range(4):
                    ps, rps = psr.next()
                    MM(K, ps[:, 0:T], Wukv[:, hp * 128:(hp + 1) * 128], CKN[:, 0:T], True, True, [rWukv, rCKN], [rps])
                    o, ro = ob.next()
                    CP(K, o[:, 0:T], ps[:, 0:T], [rps], [ro], eng="act")
                    for hh in range(2):
                        S.dma("pool", SC["KT"][b, hp * 2 + hh, 0:64, u0:u0 + T], o[hh * 64:(hh + 1) * 64, 0:T], r=[ro])
                for tb in range(nblk):
                    ps, rps = psr.next()
                    MM(K, ps[:, 0:512], CKN[:, tb * 128:(tb + 1) * 128], Wukv[:, 512:1024], True, True, [rWukv, rCKN], [rps])
                    CP(K, VT[:, tb, :, 0:64], ps[:, 0:512].rearrange("p (h c) -> p h c", c=64), [rps], [rVT])
                S.dma("pool", SC["V"][b, u0:u0 + T].rearrange("(n p) h c -> p n (h c)", p=128), VT[:, 0:nblk].rearrange("p n h c -> p n (h c)"), r=[rVT])
                ps, rps = psr.next()
                proj(128, 32, ps[0:32, 0:T], rps)
                o, ro = ob.next()
                if isc:
                    CP(K, o[0:32, 0:T], ps[0:32, 0:T], [rps], [ro], eng="act")
                else:
                    ps2, rps2 = psr.next()
                    proj(1472, 32, ps2[0:32, 0:T], rps2)
                    TT(K, T1[0:32, 0:T], ps[0:32, 0:T], COS[0:32, pos0:pos0 + T], ALU.mult, [rps, rCOS], [rT1])
                    TT(K, T2[0:32, 0:T], ps2[0:32, 0:T], SIN[0:32, pos0:pos0 + T], ALU.mult, [rps2, rSIN], [rT2])
                    TT(K, o[0:32, 0:T], T1[0:32, 0:T], T2[0:32, 0:T], ALU.add, [rT1, rT2], [ro])
                for h in range(8):
                    S.dma("pool", SC["KT"][b, h, 64:96, u0:u0 + T], o[0:32, 0:T], r=[ro])
                ps, rps = psr.next()
                proj(160, 128, ps[:, 0:T], rps)
                o, ro = ob.next()
                CP(K, o[:, 0:T], ps[:, 0:T], [rps], [ro], eng="act")
                S.dma("pool", SC["GKT"][b, :, u0:u0 + T], o[:, 0:T], r=[ro])
                for tb in range(nblk):
                    ps, rps = psr.next()
                    for k in range(8):
                        MM(K, ps[:, 0:384], H[:, k, tb * 128:(tb + 1) * 128], Win[:, k, 160:544], k == 0, k == 7,
                           [rWin, rH], [rps])
                    CP(K, GKVt[:, tb, :], ps[:, 0:384], [rps], [rGKVt])
                S.dma("pool", SC["GKV"][b, u0:u0 + T].rearrange("(n p) c -> p n c", p=128), GKVt[:, 0:nblk], r=[rGKVt])
                ps, rps = psr.next()
                proj(544, 32, ps[0:32, 0:T], rps)
                o, ro = ob.next()
                CP(K, o[0:32, 0:T], ps[0:32, 0:T], [rps], [ro], eng="act")
                S.dma("pool", SC["LRT"][b, 0, :, u0:u0 + T], o[0:16, 0:T], r=[ro])
                S.dma("pool", SC["LRT"][b, 1, :, u0:u0 + T], o[16:32, 0:T], r=[ro])
                for c in range(2):
                    ps, rps = psr.next()
                    proj(576 + c * 128, 128, ps[:, 0:T], rps)
                    o, ro = of.next()
                    CP(K, o[:, 0:T], ps[:, 0:T], [rps], [ro], eng="act")
                    S.dma("pool", SC["POOLT"][b, c * 128:(c + 1) * 128, u0:u0 + T], o[:, 0:T], r=[ro])
                psa, rpsa = psr.next()
                proj(832, 128, psa[:, 0:T], rpsa)
                psb, rpsb = psr.next()
                proj(960, 128, psb[:, 0:T], rpsb)
                ACT(K, SQ2[:, 0, 0:T], psa[:, 0:T], AF.Square, [rpsa], [rSQ2])
                ACT(K, SQ2[:, 1, 0:T], psb[:, 0:T], AF.Square, [rpsb], [rSQ2])
                ps2, rps2 = psr.next()
                for k in range(2):
                    MM(K, ps2[:, 0:T], K.ONESB[:, :], SQ2[:, k, 0:T], k == 0, k == 1, [rSQ2, K.rONESB], [rps2])
                rstd_from_psum(K, RS2[:, 0:T], ps2[:, 0:T], 256, [rps2], [rRS2])
                STT(K, CQN[:, 0, 0:T], psa[:, 0:T], qn[:, l * 2:l * 2 + 1], RS2[:, 0:T], ALU.mult, ALU.mult,
                    [rpsa, rqn, rRS2], [rCQN])
                STT(K, CQN[:, 1, 0:T], psb[:, 0:T], qn[:, l * 2 + 1:l * 2 + 2], RS2[:, 0:T], ALU.mult, ALU.mult,
                    [rpsb, rqn, rRS2], [rCQN])
                for hp in range(4):
                    ps, rps = psr.next()
                    for k in range(2):
                        MM(K, ps[:, 0:T], Wuq[:, k, hp * 128:(hp + 1) * 128], CQN[:, k, 0:T], k == 0, k == 1, [rWuq, rCQN], [rps])
                    o, ro = ob.next()
                    CP(K, o[:, 0:T], ps[:, 0:T], [rps], [ro], eng="act")
                    for hh in range(2):
                        S.dma("pool", SC["QT"][b, hp * 2 + hh, 0:64, u0:u0 + T], o[hh * 64:(hh + 1) * 64, 0:T], r=[ro])
                for g in range(2):
                    ps, rps = psr.next()
                    for k in range(2):
                        MM(K, ps[:, 0:T], Wuq[:, k, 512 + g * 128:512 + (g + 1) * 128], CQN[:, k, 0:T], k == 0, k == 1,
                           [rWuq, rCQN], [rps])
                    o, ro = ob.next()
                    if isc:
                        CP(K, o[:, 0:T], ps[:, 0:T], [rps], [ro], eng="act")
                    else:
                        ps2, rps2 = psr.next()
                        for k in range(2):
                            MM(K, ps2[:, 0:T], Wuq[:, k, 768 + g * 128:768 + (g + 1) * 128], CQN[:, k, 0:T], k == 0, k == 1,
                               [rWuq, rCQN], [rps2])
                        TT(K, T1[:, 0:T], ps[:, 0:T], COS[:, pos0:pos0 + T], ALU.mult, [rps, rCOS], [rT1])
                        TT(K, T2[:, 0:T], ps2[:, 0:T], SIN[:, pos0:pos0 + T], ALU.mult, [rps2, rSIN], [rT2])
                        TT(K, o[:, 0:T], T1[:, 0:T], T2[:, 0:T], ALU.add, [rT1, rT2], [ro])
                    for hh in range(4):
                        S.dma("pool", SC["QT"][b, g * 4 + hh, 64:96, u0:u0 + T], o[hh * 32:(hh + 1) * 32, 0:T], r=[ro])
                ps, rps = psr.next()
                proj(1088, 128, ps[:, 0:T], rps)
                o, ro = ob.next()
                CP(K, o[:, 0:T], ps[:, 0:T], [rps], [ro], eng="act")
                S.dma("pool", SC["GQT"][b, :, u0:u0 + T], o[:, 0:T], r=[ro])
                for c in range(2):
                    ps, rps = psr.next()
                    proj(1216 + c * 128, 128, ps[:, 0:T], rps)
                    o, ro = ob.next()
                    CP(K, o[:, 0:T], ps[:, 0:T], [rps], [ro], eng="act")
                    S.dma("pool", SC["OGT"][b, c * 128:(c + 1) * 128, u0:u0 + T], o[:, 0:T], r=[ro])


def phase_pool(K, l):
    nc, S, I, SC = K.nc, K.S, K.I, K.SC
    with Phase(K, "P%d" % l) as P:
        WP, rWP = P.sb("WP", [128, 2, 128], BF16)
        for c in range(2):
            load_w(K, P, WP[:, c, :], rWP, I["w_pool_bd"][l, c], 128, 128, three_d=False)
        psc, rpsc = P.sb("psc", [128, DEPTH * 2], F32)
        S.dma("sp", psc[:], I["pool_scaleT"], w=[rpsc])
        Ub = Rot(P.sbn("U", [128, 2, 528], F32, 2))
        IVb = Rot(P.sbn("IV", [128, 2, 512], F32, 2))
        A_, rA = P.sb("A", [128, 528], F32)
        B_, rB = P.sb("B", [128, 528], F32)
        C_, rC = P.sb("C", [128, 528], F32)
        E_, rE = P.sb("E", [128, 528], F32)
        Mn, rMn = P.sb("Mn", [128, 512], F32)
        Df, rDf = P.sb("Df", [128, 2, 512], BF16)
        yb = Rot(P.sbn("Y", [128, 512], BF16, 3))
        psr = Rot(list(zip(K.PS, K.rPS)))
        for b in range(NB):
            for (u0, T, isc, pos0) in tiles_of_batch():
                s0, slen = (0, LC) if isc else (LC, L)
                U, rU = Ub.next()
                IV, rIV = IVb.next()
                lo = max(u0 - 8, s0)
                hi = min(u0 + T + 8, s0 + slen)
                if lo > u0 - 8:
                    MSET(K, U[:, :, 0:8], 0.0, [rU])
                if hi < u0 + T + 8:
                    MSET(K, U[:, :, T + 8:T + 16], 0.0, [rU])
                S.dma("sp", U[:, :, lo - (u0 - 8):hi - (u0 - 8)],
                      SC["POOLT"][b].rearrange("(c p) t -> p c t", p=128)[:, :, lo:hi], w=[rU])
                S.dma("sp", IV[:, :, 0:T], I["invcnt"].rearrange("(c p) t -> p c t", p=128)[:, :, u0:u0 + T], w=[rIV])
                W = T + 16
                for c in range(2):
                    u = U[:, c, :]
                    TT(K, A_[:, 1:W], u[:, 0:W - 1], u[:, 1:W], ALU.add, [rU], [rA])
                    if c == 0:
                        TT(K, B_[64:128, 2:W - 1], A_[64:128, 1:W - 2], A_[64:128, 3:W], ALU.add, [rA], [rB])
                        TT(K, Mn[0:64, 0:T], A_[0:64, 8:8 + T], IV[0:64, c, 0:T], ALU.mult, [rA, rIV], [rMn])
                        TT(K, Mn[64:128, 0:T], B_[64:128, 8:8 + T], IV[64:128, c, 0:T], ALU.mult, [rB, rIV], [rMn])
                    else:
                        TT(K, B_[:, 2:W - 1], A_[:, 1:W - 2], A_[:, 3:W], ALU.add, [rA], [rB])
                        TT(K, C_[:, 4:W - 3], B_[:, 2:W - 5], B_[:, 6:W - 1], ALU.add, [rB], [rC])
                        TT(K, E_[64:128, 8:W - 7], C_[64:128, 4:W - 11], C_[64:128, 12:W - 3], ALU.add, [rC], [rE])
                        TT(K, Mn[0:64, 0:T], C_[0:64, 8:8 + T], IV[0:64, c, 0:T], ALU.mult, [rC, rIV], [rMn])
                        TT(K, Mn[64:128, 0:T], E_[64:128, 8:8 + T], IV[64:128, c, 0:T], ALU.mult, [rE, rIV], [rMn])
                    TT(K, Df[:, c, 0:T], Mn[:, 0:T], u[:, 8:8 + T], ALU.subtract, [rMn, rU], [rDf])
                    ps, rps = psr.next()
                    MM(K, ps[:, 0:T], WP[:, c, :], Df[:, c, 0:T], True, True, [rWP, rDf], [rps])
                    Y, rY = yb.next()
                    ACT(K, Y[:, 0:T], ps[:, 0:T], AF.Identity, [rps, rpsc], [rY], scale=psc[:, l * 2 + c:l * 2 + c + 1])
                    S.dma("pool", SC["YT"][b, c * 128:(c + 1) * 128, u0:u0 + T], Y[:, 0:T], r=[rY])


def phase_mla(K, l):
    nc, S, I, SC = K.nc, K.S, K.I, K.SC
    scale = float(96 ** -0.5)
    NKC = NT // 128
    with Phase(K, "M%d" % l) as P:
        VB, rVB = P.sb("VB", [128, NKC, 8, 128], BF16)
        KTb = Rot(P.sbn("KT", [96, NT], BF16, 2))
        QTb = Rot(P.sbn("QT", [96, 512], BF16, 2))
        PTb = Rot(P.sbn("PT", [128, 512], BF16, 4))
        R_, rR = P.sb("R", [64, 512], F32)
        Yb = Rot(P.sbn("Y", [64, 512], BF16, 2))
        obank = Rot([(K.PS[i], K.rPS[i]) for i in (0, 1)])
        sbank = Rot([(K.PS[i], K.rPS[i]) for i in (2, 3, 4, 5, 6, 7)])
        for b in range(NB):
            vsrc = SC["V"][b].rearrange("(n p) h c -> p n (h c)", p=128)
            vdst = VB[:].rearrange("p n h c -> p n (h c)")
            for n0 in range(0, NKC, 6):
                n1 = min(NKC, n0 + 6)
                S.dma("sp", vdst[:, n0:n1], vsrc[:, n0:n1], w=[rVB])
            for h in range(8):
                KT, rKT = KTb.next()
                S.dma("sp", KT[:, :], SC["KT"][b, h], w=[rKT])
                for (u0, T, isc, pos0) in tiles_of_batch():
                    nk = (LC // 128) if isc else NKC
                    QT, rQT = QTb.next()
                    S.dma("sp", QT[:, 0:T], SC["QT"][b, h, :, u0:u0 + T], w=[rQT])
                    O, rO = obank.next()
                    pend = None
                    sps = {}

                    def smm(kc):
                        ps, rps = sbank.next()
                        MM(K, ps[:, 0:T], KT[:, kc * 128:(kc + 1) * 128], QT[:, 0:T], True, True, [rKT, rQT], [rps])
                        sps[kc] = (ps, rps)

                    smm(0)
                    for kc in range(nk):
                        if kc + 1 < nk:
                            smm(kc + 1)
                        ps, rps = sps.pop(kc)
                        PT, rPT = PTb.next()
                        ACT(K, PT[:, 0:T], ps[:, 0:T], AF.Exp, [rps], [rPT], scale=scale)
                        MM(K, O[:, 0:T], VB[:, kc, h, :], PT[:, 0:T], kc == 0, kc == nk - 1, [rVB, rPT], [rO])
                    CP(K, R_[0:64, 0:T], O[64:128, 0:T], [rO], [rR], eng="act")
                    RECIP(K, R_[0:64, 0:T], R_[0:64, 0:T], [rR], [rR])
                    Y, rY = Yb.next()
                    TT(K, Y[0:64, 0:T], O[0:64, 0:T], R_[0:64, 0:T], ALU.mult, [rO, rR], [rY])
                    S.dma("pool", SC["YT"][b, 256 + h * 64:256 + (h + 1) * 64, u0:u0 + T], Y[0:64, 0:T], r=[rY])


def phase_gla(K, l):
    nc, S, I, SC = K.nc, K.S, K.I, K.SC
    qs = float(32 ** -0.5)
    with Phase(K, "G%d" % l) as P:
        Wgk, rWgk = P.sb("Wgk", [128, 2, 128], BF16)
        MSET(K, Wgk[:], 0.0, [rWgk])
        for d in range(2):
            load_w(K, P, Wgk[:, d, :], rWgk, I["w_gk"][l, d], 16, 128, three_d=False)
        BG, rBG = P.sb("BG", [128, 2, 128], F32)
        for d in range(2):
            S.dma("sp", BG[:, d, :], I["b_gk"][l, d:d + 1, :].partition_broadcast(128), w=[rBG])
        TRI, rTRI = P.sb("TRI", [128, 2, 128], F32)
        MSK, rMSK = P.sb("MSK", [128, 2, 512], F32)
        for d in range(2):
            S.dma("sp", TRI[:, d, :], I["tri"][d], w=[rTRI])
            S.dma("sp", MSK[:, d, :], I["mask"][d], w=[rMSK])
        HM, rHM = P.sb("HM", [128, 4], F32)
        BLK, rBLK = P.sb("BLK", [128, 256], F32)
        S.dma("sp", HM[:], I["headmask"], w=[rHM])
        S.dma("sp", BLK[:], I["blkmask"], w=[rBLK])
        O64f, rO64f = P.sb("O64f", [128, 128], F32)
        O64, rO64 = P.sb("O64", [128, 128], BF16)
        S.dma("sp", O64f[:], I["ones64"], w=[rO64f])
        CP(K, O64[:], O64f[:], [rO64f], [rO64])
        gn, rgn = P.sb("gn", [128, DEPTH], F32)
        S.dma("sp", gn[:], I["gla_normT"], w=[rgn])

        GKVb = Rot(P.sbn("GKV", [128, 384], BF16, 2))
        GKTb = Rot(P.sbn("GKT", [128, 128], BF16, 2))
        GQTb = Rot(P.sbn("GQT", [128, 128], BF16, 2))
        LRb = Rot(P.sbn("LR", [128, 128], BF16, 2))
        for (t_, r_) in LRb.items:
            MSET(K, t_[:], 0.0, [r_])
        OGb = Rot(P.sbn("OG", [128, 2, 128], BF16, 2))
        OFb = Rot(P.sbn("OFl", [128, 2, 128], F32, 2))
        XB, rXB = P.sb("XB", [128, 128], F32)
        E1, rE1 = P.sb("E1", [128, 128], F32)
        LL, rLL = P.sb("LL", [128, 128], F32)
        EBT, rEBT = P.sb("EBT", [128, 128], F32)
        ENBT, rENBT = P.sb("ENBT", [128, 128], F32)
        ENB, rENB = P.sb("ENB", [128, 128], F32)
        QBT, rQBT = P.sb("QBT", [128, 128], BF16)
        KBT, rKBT = P.sb("KBT", [128, 128], BF16)
        KB, rKB = P.sb("KB", [128, 128], BF16)
        QBX, rQBX = P.sb("QBX", [128, 4, 128], BF16)
        ATM, rATM = P.sb("ATM", [128, 512], BF16)
        ST, rST = P.sb("ST", [128, 256], F32)
        STb, rSTb = P.sb("STb", [128, 256], BF16)
        DUM, rDUM = P.sb("DUM", [128, 256], F32)
        OS, rOS = P.sb("OS", [128, 2, 128], F32)
        SQ, rSQ = P.sb("SQ", [128, 2, 128], BF16)
        RS, rRS = P.sb("RS", [128, 2, 128], F32)
        SG, rSG = P.sb("SG", [128, 2, 128], F32)
        Yb = Rot(P.sbn("Y", [128, 2, 128], BF16, 2))
        psr = Rot(list(zip(K.PS, K.rPS)))

        utiles = [i * 128 for i in range(NT // 128)]
        ctx_t = utiles[:LC // 128]
        lat_t = utiles[LC // 128:]
        for b in range(NB):
            for d in range(2):
                order = (ctx_t + lat_t) if d == 0 else (ctx_t[::-1] + lat_t[::-1])
                if d == 1:
                    S.barrier()
                MSET(K, ST[:], 0.0, [rST])
                MSET(K, STb[:], 0.0, [rSTb])
                for u0 in order[:K.gla_nt]:
                    GKV, rGKV = GKVb.next()
                    GKT, rGKT = GKTb.next()
                    GQT, rGQT = GQTb.next()
                    LR, rLR = LRb.next()
                    S.dma("sp", GKV[:], SC["GKV"][b, u0:u0 + 128, :], w=[rGKV])
                    S.dma("sp", GKT[:], SC["GKT"][b, :, u0:u0 + 128], w=[rGKT])
                    S.dma("sp", GQT[:], SC["GQT"][b, :, u0:u0 + 128], w=[rGQT])
                    S.dma("sp", LR[0:16, :], SC["LRT"][b, d, :, u0:u0 + 128], w=[rLR])
                    if d == 1:
                        OG, rOG = OGb.next()
                        OFl, rOFl = OFb.next()
                        S.dma("sp", OG[:], SC["OGT"][b].rearrange("(c p) t -> p c t", p=128)[:, :, u0:u0 + 128], w=[rOG])
                        S.dma("sp", OFl[:], SC["OF"][b].rearrange("(c p) t -> p c t", p=128)[:, :, u0:u0 + 128], w=[rOFl])
                    ps, rps = psr.next()
                    MM(K, ps[:, 0:128], LR[:, :], Wgk[:, d, :], True, True, [rLR, rWgk], [rps])
                    TT(K, XB[:], ps[:, 0:128], BG[:, d, :], ALU.add, [rps, rBG], [rXB])
                    ACT(K, E1[:], XB[:], AF.Exp, [rXB], [rE1], scale=-1.0)
                    ACT(K, LL[:], E1[:], AF.Ln, [rE1], [rLL], bias=K.one_col[:, 0:1])
                    if K.gla_lvl < 2:
                        continue
                    psb, rpsb = psr.next()
                    MM(K, psb[:, 0:128], LL[:], TRI[:, d, :], True, True, [rLL, rTRI], [rpsb])
                    MM(K, psb[:, 128:256], TRI[:, d, :], LL[:], True, True, [rLL, rTRI], [rpsb])
                    ACT(K, EBT[:], psb[:, 0:128], AF.Exp, [rpsb], [rEBT])
                    ACT(K, ENBT[:], psb[:, 0:128], AF.Exp, [rpsb], [rENBT], scale=-1.0)
                    ACT(K, ENB[:], psb[:, 128:256], AF.Exp, [rpsb], [rENB], scale=-1.0)
                    if K.gla_lvl < 3:
                        continue
                    STT(K, QBT[:], GQT[:], qs, EBT[:], ALU.mult, ALU.mult, [rGQT, rEBT], [rQBT])
                    TT(K, KBT[:], GKT[:], ENBT[:], ALU.mult, [rGKT, rENBT], [rKBT])
                    TT(K, KB[:], GKV[:, 0:128], ENB[:], ALU.mult, [rGKV, rENB], [rKB])
                    if K.gla_x == 6:
                        for h in range(4):
                            ACT(K, QBX[:, h, :], QBT[:], AF.Identity, [rQBT, rHM], [rQBX], scale=HM[:, h:h + 1])
                    else:
                        TT(K, QBX[:], QBT[:].unsqueeze(1).broadcast_to([128, 4, 128]),
                           HM[:].unsqueeze(2).broadcast_to([128, 4, 128]), ALU.mult, [rQBT, rHM], [rQBX])
                    if K.gla_x == 5:
                        psr.next()
                    if K.gla_lvl < 4:
                        continue
                    psa, rpsa = psr.next()
                    if K.gla_x not in (2, 3, 4, 7):
                        MM(K, psa[:, 0:512], KBT[:], QBX[:].rearrange("p h i -> p (h i)"), True, True, [rKBT, rQBX], [rpsa])
                    if K.gla_x == 7:
                        TT(K, E1[:], ENB[:], ENB[:], ALU.mult, [rENB], [rE1])
                    elif K.gla_x == 3:
                        TT(K, ATM[:], MSK[:, d, :], MSK[:, d, :], ALU.mult, [rMSK], [rATM])
                    elif K.gla_x == 4:
                        TT(K, ATM[:, 0:128], psa[:, 0:128], ENB[:], ALU.mult, [rpsa, rENB], [rATM])
                    elif K.gla_x != 1:
                        TT(K, ATM[:], psa[:, 0:512], MSK[:, d, :], ALU.mult, [rpsa, rMSK], [rATM])
                    if K.gla_lvl < 5:
                        continue
                    psus = [psr.next(), psr.next()]
                    for c in range(2):
                        MM(K, psus[c][0][:, 0:256], KB[c * 64:(c + 1) * 64, :], GKV[c * 64:(c + 1) * 64, 128:384],
                           True, True, [rKB, rGKV], [psus[c][1]])
                    if K.gla_lvl < 6:
                        continue
                    pso, rpso = psr.next()
                    corder = (0, 1) if d == 0 else (1, 0)
                    first = True
                    for c in corder:
                        for hp in range(2):
                            MM(K, pso[:, hp * 128 + c * 64:hp * 128 + (c + 1) * 64], STb[:, hp * 128:(hp + 1) * 128],
                               QBT[:, c * 64:(c + 1) * 64], first, False, [rSTb, rQBT], [rpso], skip=True)
                            first = False
                        dcol = (c * 64 + 63) if d == 0 else (c * 64)
                        STT(K, DUM[:], psus[c][0][:, 0:256], EBT[:, dcol:dcol + 1], BLK[:], ALU.mult, ALU.mult,
                            [psus[c][1], rEBT, rBLK], [rDUM])
                        STT(K, ST[:], ST[:], EBT[:, dcol:dcol + 1], DUM[:], ALU.mult, ALU.add, [rST, rEBT, rDUM], [rST])
                        CP(K, STb[:], ST[:], [rST], [rSTb], eng="act")
                    for h in range(4):
                        MM(K, pso[(h % 2) * 64:(h % 2) * 64 + 64, (h // 2) * 128:(h // 2) * 128 + 128],
                           GKV[:, 128 + h * 64:128 + (h + 1) * 64], ATM[:, h * 128:(h + 1) * 128], False, h == 3,
                           [rGKV, rATM], [rpso], skip=True)
                    if K.gla_lvl < 7:
                        continue
                    if d == 0:
                        CP(K, OS[:], pso[:, 0:256].rearrange("p (c i) -> p c i", c=2), [rpso], [rOS], eng="act")
                        S.dma("pool", SC["OF"][b].rearrange("(c p) t -> p c t", p=128)[:, :, u0:u0 + 128], OS[:], r=[rOS])
                    else:
                        TT(K, OS[:], pso[:, 0:256].rearrange("p (c i) -> p c i", c=2), OFl[:], ALU.add, [rpso, rOFl], [rOS])
                        ACT(K, SQ[:], OS[:], AF.Square, [rOS], [rSQ])
                        psn, rpsn = psr.next()
                        for c in range(2):
                            MM(K, psn[:, c * 128:(c + 1) * 128], O64[:], SQ[:, c, :], True, True, [rO64, rSQ], [rpsn], skip=True)
                        rstd_from_psum(K, RS[:], psn[:, 0:256].rearrange("p (c i) -> p c i", c=2), 64, [rpsn], [rRS])
                        ACT(K, SG[:], OG[:], AF.Silu, [rOG], [rSG])
                        TT(K, RS[:], RS[:], SG[:], ALU.mult, [rRS, rSG], [rRS])
                        Y, rY = Yb.next()
                        STT(K, Y[:], OS[:], gn[:, l:l + 1], RS[:], ALU.mult, ALU.mult, [rOS, rgn, rRS], [rY])
                        S.dma("pool", SC["YT"][b, 768:1024].rearrange("(c p) t -> p c t", p=128)[:, :, u0:u0 + 128], Y[:], r=[rY])


def gla_gen(K, P, l, banks, blist):
    nc, S, I, SC = K.nc, K.S, K.I, K.SC
    qs = float(32 ** -0.5)
    Wgk, rWgk = P.sb("Wgk", [128, 2, 128], BF16)
    MSET(K, Wgk[:], 0.0, [rWgk])
    for d in range(2):
        load_w(K, P, Wgk[:, d, :], rWgk, I["w_gk"][l, d], 16, 128, three_d=False)
    BG, rBG = P.sb("BG", [128, 2, 128], F32)
    for d in range(2):
        S.dma("sp", BG[:, d, :], I["b_gk"][l, d:d + 1, :].partition_broadcast(128), w=[rBG])
    TRI, rTRI = P.sb("TRI", [128, 2, 128], F32)
    MSK, rMSK = P.sb("MSK", [128, 2, 512], F32)
    for d in range(2):
        S.dma("sp", TRI[:, d, :], I["tri"][d], w=[rTRI])
        S.dma("sp", MSK[:, d, :], I["mask"][d], w=[rMSK])
    HM, rHM = P.sb("HM", [128, 4], F32)
    BLK, rBLK = P.sb("BLK", [128, 256], F32)
    S.dma("sp", HM[:], I["headmask"], w=[rHM])
    S.dma("sp", BLK[:], I["blkmask"], w=[rBLK])
    O64f, rO64f = P.sb("O64f", [128, 128], F32)
    O64, rO64 = P.sb("O64", [128, 128], BF16)
    S.dma("sp", O64f[:], I["ones64"], w=[rO64f])
    CP(K, O64[:], O64f[:], [rO64f], [rO64])
    gn, rgn = P.sb("gn", [128, DEPTH], F32)
    S.dma("sp", gn[:], I["gla_normT"], w=[rgn])

    GKVb = Rot(P.sbn("GKV", [128, 384], BF16, 3))
    GKTb = Rot(P.sbn("GKT", [128, 128], BF16, 3))
    GQTb = Rot(P.sbn("GQT", [128, 128], BF16, 3))
    LRb = Rot(P.sbn("LR", [128, 128], BF16, 3))
    for (t_, r_) in LRb.items:
        MSET(K, t_[:], 0.0, [r_])
    OGb = Rot(P.sbn("OG", [128, 2, 128], BF16, 3))
    OFb = Rot(P.sbn("OFl", [128, 2, 128], F32, 3))
    XB, rXB = P.sb("XB", [128, 128], F32)
    E1, rE1 = P.sb("E1", [128, 128], F32)
    LL, rLL = P.sb("LL", [128, 128], F32)
    EBT, rEBT = P.sb("EBT", [128, 128], F32)
    ENBT, rENBT = P.sb("ENBT", [128, 128], F32)
    ENB, rENB = P.sb("ENB", [128, 128], F32)
    QBT, rQBT = P.sb("QBT", [128, 128], BF16)
    KBT, rKBT = P.sb("KBT", [128, 128], BF16)
    KB, rKB = P.sb("KB", [128, 128], BF16)
    QBX, rQBX = P.sb("QBX", [128, 4, 128], BF16)
    ATM, rATM = P.sb("ATM", [128, 512], BF16)
    ST, rST = P.sb("ST", [128, 256], F32)
    STb, rSTb = P.sb("STb", [128, 256], BF16)
    DUM2, rDUM2 = P.sb("DUM2", [128, 2, 256], F32)
    OS, rOS = P.sb("OS", [128, 2, 128], F32)
    SQ, rSQ = P.sb("SQ", [128, 2, 128], BF16)
    RS, rRS = P.sb("RS", [128, 2, 128], F32)
    SG, rSG = P.sb("SG", [128, 2, 128], F32)
    Yb = Rot(P.sbn("Y", [128, 2, 128], BF16, 2))
    psr = Rot([(K.PS[i], K.rPS[i]) for i in banks])

    utiles = [i * 128 for i in range(NT // 128)]
    ctx_t = utiles[:LC // 128]
    lat_t = utiles[LC // 128:]
    for d in range(2):
        if d == 1:
            yield "BARRIER"
        for b in blist:
            order = (ctx_t + lat_t) if d == 0 else (ctx_t[::-1] + lat_t[::-1])
            MSET(K, ST[:], 0.0, [rST])
            MSET(K, STb[:], 0.0, [rSTb])
            def issue_loads(u0_):
                GKV_, rGKV_ = GKVb.next()
                GKT_, rGKT_ = GKTb.next()
                GQT_, rGQT_ = GQTb.next()
                LR_, rLR_ = LRb.next()
                S.dma("sp", GKV_[:], SC["GKV"][b, u0_:u0_ + 128, :], w=[rGKV_])
                S.dma("sp", GKT_[:], SC["GKT"][b, :, u0_:u0_ + 128], w=[rGKT_])
                S.dma("sp", GQT_[:], SC["GQT"][b, :, u0_:u0_ + 128], w=[rGQT_])
                S.dma("sp", LR_[0:16, :], SC["LRT"][b, d, :, u0_:u0_ + 128], w=[rLR_])
                og = None
                if d == 1:
                    OG_, rOG_ = OGb.next()
                    OFl_, rOFl_ = OFb.next()
                    S.dma("sp", OG_[:], SC["OGT"][b].rearrange("(c p) t -> p c t", p=128)[:, :, u0_:u0_ + 128], w=[rOG_])
                    S.dma("sp", OFl_[:], SC["OF"][b].rearrange("(c p) t -> p c t", p=128)[:, :, u0_:u0_ + 128], w=[rOFl_])
                    og = (OG_, rOG_, OFl_, rOFl_)
                return (GKV_, rGKV_, GKT_, rGKT_, GQT_, rGQT_, LR_, rLR_, og)

            pending = issue_loads(order[0])
            for ui, u0 in enumerate(order):
                (GKV, rGKV, GKT, rGKT, GQT, rGQT, LR, rLR, og) = pending
                if og is not None:
                    (OG, rOG, OFl, rOFl) = og
                if ui + 1 < len(order):
                    pending = issue_loads(order[ui + 1])
                ps, rps = psr.next()
                MM(K, ps[:, 0:128], LR[:, :], Wgk[:, d, :], True, True, [rLR, rWgk], [rps])
                TT(K, XB[:], ps[:, 0:128], BG[:, d, :], ALU.add, [rps, rBG], [rXB])
                ACT(K, E1[:], XB[:], AF.Exp, [rXB], [rE1], scale=-1.0)
                ACT(K, LL[:], E1[:], AF.Ln, [rE1], [rLL], bias=K.one_col[:, 0:1])
                yield None
                psb, rpsb = psr.next()
                MM(K, psb[:, 0:128], LL[:], TRI[:, d, :], True, True, [rLL, rTRI], [rpsb])
                MM(K, psb[:, 128:256], TRI[:, d, :], LL[:], True, True, [rLL, rTRI], [rpsb])
                ACT(K, EBT[:], psb[:, 0:128], AF.Exp, [rpsb], [rEBT])
                ACT(K, ENBT[:], psb[:, 0:128], AF.Exp, [rpsb], [rENBT], scale=-1.0)
                ACT(K, ENB[:], psb[:, 128:256], AF.Exp, [rpsb], [rENB], scale=-1.0)
                STT(K, QBT[:], GQT[:], qs, EBT[:], ALU.mult, ALU.mult, [rGQT, rEBT], [rQBT])
                TT(K, KBT[:], GKT[:], ENBT[:], ALU.mult, [rGKT, rENBT], [rKBT])
                TT(K, KB[:], GKV[:, 0:128], ENB[:], ALU.mult, [rGKV, rENB], [rKB])
                TT(K, QBX[:], QBT[:].unsqueeze(1).broadcast_to([128, 4, 128]),
                   HM[:].unsqueeze(2).broadcast_to([128, 4, 128]), ALU.mult, [rQBT, rHM], [rQBX])
                yield None
                psa, rpsa = psr.next()
                MM(K, psa[:, 0:512], KBT[:], QBX[:].rearrange("p h i -> p (h i)"), True, True, [rKBT, rQBX], [rpsa])
                TT(K, ATM[:], psa[:, 0:512], MSK[:, d, :], ALU.mult, [rpsa, rMSK], [rATM])
                psus = [psr.next(), psr.next()]
                for c in range(2):
                    MM(K, psus[c][0][:, 0:256], KB[c * 64:(c + 1) * 64, :], GKV[c * 64:(c + 1) * 64, 128:384],
                       True, True, [rKB, rGKV], [psus[c][1]])
                for c in range(2):
                    dcol = (c * 64 + 63) if d == 0 else (c * 64)
                    STT(K, DUM2[:, c, :], psus[c][0][:, 0:256], EBT[:, dcol:dcol + 1], BLK[:], ALU.mult, ALU.mult,
                        [psus[c][1], rEBT, rBLK], [rDUM2])
                yield None
                pso, rpso = psr.next()
                corder = (0, 1) if d == 0 else (1, 0)
                first = True
                for c in corder:
                    for hp in range(2):
                        MM(K, pso[:, hp * 128 + c * 64:hp * 128 + (c + 1) * 64], STb[:, hp * 128:(hp + 1) * 128],
                           QBT[:, c * 64:(c + 1) * 64], first, False, [rSTb, rQBT], [rpso], skip=True)
                        first = False
                    dcol = (c * 64 + 63) if d == 0 else (c * 64)
                    STT(K, ST[:], ST[:], EBT[:, dcol:dcol + 1], DUM2[:, c, :], ALU.mult, ALU.add, [rST, rEBT, rDUM2], [rST])
                    CP(K, STb[:], ST[:], [rST], [rSTb], eng="act")
                yield None
                for h in range(4):
                    MM(K, pso[(h % 2) * 64:(h % 2) * 64 + 64, (h // 2) * 128:(h // 2) * 128 + 128],
                       GKV[:, 128 + h * 64:128 + (h + 1) * 64], ATM[:, h * 128:(h + 1) * 128], False, h == 3,
                       [rGKV, rATM], [rpso], skip=True)
                yield None
                if d == 0:
                    CP(K, OS[:], pso[:, 0:256].rearrange("p (c i) -> p c i", c=2), [rpso], [rOS], eng="act")
                    S.dma("pool", SC["OF"][b].rearrange("(c p) t -> p c t", p=128)[:, :, u0:u0 + 128], OS[:], r=[rOS])
                else:
                    TT(K, OS[:], pso[:, 0:256].rearrange("p (c i) -> p c i", c=2), OFl[:], ALU.add, [rpso, rOFl], [rOS])
                    ACT(K, SQ[:], OS[:], AF.Square, [rOS], [rSQ])
                    psn, rpsn = psr.next()
                    for c in range(2):
                        MM(K, psn[:, c * 128:(c + 1) * 128], O64[:], SQ[:, c, :], True, True, [rO64, rSQ], [rpsn], skip=True)
                    rstd_from_psum(K, RS[:], psn[:, 0:256].rearrange("p (c i) -> p c i", c=2), 64, [rpsn], [rRS])
                    ACT(K, SG[:], OG[:], AF.Silu, [rOG], [rSG])
                    TT(K, RS[:], RS[:], SG[:], ALU.mult, [rRS, rSG], [rRS])
                    Y, rY = Yb.next()
                    STT(K, Y[:], OS[:], gn[:, l:l + 1], RS[:], ALU.mult, ALU.mult, [rOS, rgn, rRS], [rY])
                    S.dma("pool", SC["YT"][b, 768:1024].rearrange("(c p) t -> p c t", p=128)[:, :, u0:u0 + 128], Y[:], r=[rY])


def mla_gen(K, P, l, obanks, sbanks):
    nc, S, I, SC = K.nc, K.S, K.I, K.SC
    scale = float(96 ** -0.5)
    NKC = NT // 128
    VB, rVB = P.sb("VB", [128, NKC, 8, 128], BF16)
    KTb = Rot(P.sbn("KT", [96, NT], BF16, 2))
    QTb = Rot(P.sbn("QT", [96, 512], BF16, 3))
    PTb = Rot(P.sbn("PT", [128, 512], BF16, 4))
    R_, rR = P.sb("R", [64, 512], F32)
    Yb = Rot(P.sbn("Y", [64, 512], BF16, 2))
    obank = Rot([(K.PS[i], K.rPS[i]) for i in obanks])
    sbank = Rot([(K.PS[i], K.rPS[i]) for i in sbanks])
    units = [(b, h, ti, t) for b in range(NB) for h in range(8) for ti, t in enumerate(tiles_of_batch())]
    kts = {}

    def load_unit(i):
        b, h, ti, (u0, T, isc, pos0) = units[i]
        if ti == 0:
            KT_, rKT_ = KTb.next()
            S.dma("sp", KT_[:, :], SC["KT"][b, h], w=[rKT_])
            kts[(b, h)] = (KT_, rKT_)
        QT_, rQT_ = QTb.next()
        S.dma("sp", QT_[:, 0:T], SC["QT"][b, h, :, u0:u0 + T], w=[rQT_])
        return (QT_, rQT_)

    pending = load_unit(0)
    for ui, (b, h, ti, (u0, T, isc, pos0)) in enumerate(units):
        if h == 0 and ti == 0:
            vsrc = SC["V"][b].rearrange("(n p) h c -> p n (h c)", p=128)
            vdst = VB[:].rearrange("p n h c -> p n (h c)")
            for n0 in range(0, NKC, 6):
                n1 = min(NKC, n0 + 6)
                S.dma("sp", vdst[:, n0:n1], vsrc[:, n0:n1], w=[rVB])
        QT, rQT = pending
        KT, rKT = kts[(b, h)]
        if ui + 1 < len(units):
            pending = load_unit(ui + 1)
        nk = (LC // 128) if isc else NKC
        O, rO = obank.next()
        sps = {}

        def smm(kc, KT=KT, rKT=rKT, QT=QT, rQT=rQT, T=T, sps=sps):
            ps, rps = sbank.next()
            MM(K, ps[:, 0:T], KT[:, kc * 128:(kc + 1) * 128], QT[:, 0:T], True, True, [rKT, rQT], [rps])
            sps[kc] = (ps, rps)

        smm(0)
        if nk > 1:
            smm(1)
        for kc in range(nk):
            if kc + 2 < nk:
                smm(kc + 2)
            ps, rps = sps.pop(kc)
            PT, rPT = PTb.next()
            ACT(K, PT[:, 0:T], ps[:, 0:T], AF.Exp, [rps], [rPT], scale=scale)
            MM(K, O[:, 0:T], VB[:, kc, h, :], PT[:, 0:T], kc == 0, kc == nk - 1, [rVB, rPT], [rO])
            yield None
        CP(K, R_[0:64, 0:T], O[64:128, 0:T], [rO], [rR], eng="act")
        RECIP(K, R_[0:64, 0:T], R_[0:64, 0:T], [rR], [rR])
        Y, rY = Yb.next()
        TT(K, Y[0:64, 0:T], O[0:64, 0:T], R_[0:64, 0:T], ALU.mult, [rO, rR], [rY])
        S.dma("pool", SC["YT"][b, 256 + h * 64:256 + (h + 1) * 64, u0:u0 + T], Y[0:64, 0:T], r=[rY])


def phase_mix(K, l):
    with Phase(K, "X%d" % l) as P:
        gm = mla_gen(K, P, l, (0, 1), (2, 3, 4, 5, 6, 7))
        ggs = [gla_gen(K, P, l, (4, 5), [0]), gla_gen(K, P, l, (6, 7), [1])]
        alive = [True, True]
        waiting = [False, False]
        import os
        skip = os.environ.get("MIX_SKIP", "")
        if os.environ.get("MIX_MODE", "seq") == "seq":
            for _ in gm:
                pass
        if skip == "gla":
            alive = [False, False]
        n_mla = NB * 8 * ((L // 512) * (NT // 128) + LC // 128)
        n_gla = 2 * NB * (NT // 128) * 5
        ratio = n_mla / float(n_gla)
        acc = 0.0
        mla_done = (skip == "mla")
        turn = 0
        while any(alive) or not mla_done:
            if any(alive):
                if all(waiting[i] or not alive[i] for i in range(2)):
                    K.S.barrier()
                    waiting = [False, False]
                i = turn % 2
                turn += 1
                if not alive[i] or waiting[i]:
                    i = 1 - i
                if alive[i] and not waiting[i]:
                    try:
                        tok = next(ggs[i])
                        if tok == "BARRIER":
                            waiting[i] = True
                    except StopIteration:
                        alive[i] = False
            acc += ratio
            while (acc >= 1.0 or not any(alive)) and not mla_done:
                acc -= 1.0
                try:
                    next(gm)
                except StopIteration:
                    mla_done = True
                    break


def phase_D1(K, l):
    nc, S, I, SC = K.nc, K.S, K.I, K.SC
    with Phase(K, "D%d" % l) as P:
        Wo, rWo = P.sb("Wo", [128, 8, 1024], BF16)
        load_w(K, P, Wo, rWo, I["w_o"][l], 1024, 1024)
        Xb = Rot(P.sbn("X", [128, 8, 512], F32, 2))
        Yb = Rot(P.sbn("Y", [128, 8, 512], BF16, 2))
        SQ, rSQ = P.sb("SQ", [128, 8, 512], BF16)
        RS, rRS = P.sb("RS", [128, 512], F32)
        TMP, rTMP = P.sb("TMP", [128, 512], F32)
        Hb = Rot(P.sbn("H", [128, 8, 512], BF16, 2))
        psr = Rot(list(zip(K.PS, K.rPS)))
        src_x = K.x_src(l)
        for b in range(NB):
            xv = src_x[b].rearrange("(k p) t -> p k t", p=128)
            xo = SC["XT"][b].rearrange("(k p) t -> p k t", p=128)
            yv = SC["YT"][b].rearrange("(k p) t -> p k t", p=128)
            hv = SC["H2T"][b].rearrange("(k p) t -> p k t", p=128)
            for (u0, T, isc, pos0) in tiles_of_batch():
                j = 2 if isc else b
                X, rX = Xb.next()
                Y, rY = Yb.next()
                S.dma("sp", X[:, :, 0:T], xv[:, :, u0:u0 + T], w=[rX])
                S.dma("sp", Y[:, :, 0:T], yv[:, :, u0:u0 + T], w=[rY])
                for fc in range(8):
                    ps, rps = psr.next()
                    for k in range(8):
                        MM(K, ps[:, 0:T], Wo[:, k, fc * 128:(fc + 1) * 128], Y[:, k, 0:T], k == 0, k == 7, [rWo, rY], [rps])
                    STT(K, X[:, fc, 0:T], ps[:, 0:T], K.MOD[:, l, 2, fc, j:j + 1], X[:, fc, 0:T], ALU.mult, ALU.add,
                        [rps, K.rMOD, rX], [rX])
                S.dma("pool", xo[:, :, u0:u0 + T], X[:, :, 0:T], r=[rX])
                ACT(K, SQ[:, :, 0:T], X[:, :, 0:T], AF.Square, [rX], [rSQ])
                ps, rps = psr.next()
                for k in range(8):
                    MM(K, ps[:, 0:T], K.ONESB[:, :], SQ[:, k, 0:T], k == 0, k == 7, [rSQ, K.rONESB], [rps])
                rstd_from_psum(K, RS[:, 0:T], ps[:, 0:T], D, [rps], [rRS])
                H, rH = Hb.next()
                for k in range(8):
                    STT(K, TMP[:, 0:T], X[:, k, 0:T], K.MOD[:, l, 4, k, j:j + 1], RS[:, 0:T], ALU.mult, ALU.mult,
                        [rX, rRS, K.rMOD], [rTMP])
                    ACT(K, H[:, k, 0:T], TMP[:, 0:T], AF.Identity, [rTMP, K.rMOD], [rH], bias=K.MOD[:, l, 3, k, j:j + 1])
                S.dma("pool", hv[:, :, u0:u0 + T], H[:, :, 0:T], r=[rH])


def phase_D2(K, l):
    nc, S, I, SC = K.nc, K.S, K.I, K.SC
    T = 256
    with Phase(K, "F%d" % l) as P:
        Wup, rWup = P.sb("Wup", [128, 8, 2 * DFF], BF16)
        Wdn, rWdn = P.sb("Wdn", [128, NFF, 1024], BF16)
        load_w(K, P, Wup, rWup, I["w_up"][l], 1024, 2 * DFF)
        load_w(K, P, Wdn, rWdn, I["w_down"][l], DFF, 1024)
        cw, rcw = P.sb("cw", [128, DEPTH, 3, NFF], F32)
        cb, rcb = P.sb("cb", [128, DEPTH, NFF], F32)
        S.dma("sp", cw[:], I["conv_wT"], w=[rcw])
        S.dma("sp", cb[:], I["conv_bT"], w=[rcb])
        Xb = Rot(P.sbn("X", [128, 8, T], F32, 2))
        Hb = Rot(P.sbn("H", [128, 8, T + 2], BF16, 2))
        C1, rC1 = P.sb("C1", [128, T], F32)
        C2, rC2 = P.sb("C2", [128, T], F32)
        C3, rC3 = P.sb("C3", [128, T], F32)
        SL, rSL = P.sb("SL", [128, T], F32)
        A_, rA = P.sb("ACTV", [128, NFF, T], BF16)
        psr = Rot(list(zip(K.PS, K.rPS)))
        for b in range(NB):
            xo = SC["XT"][b].rearrange("(k p) t -> p k t", p=128)
            hv = SC["H2T"][b].rearrange("(k p) t -> p k t", p=128)
            for (s0, slen) in ((0, LC), (LC, L)):
                j = 2 if s0 == 0 else b
                for u0 in range(s0, s0 + slen, T):
                    X, rX = Xb.next()
                    H, rH = Hb.next()
                    lo = max(u0 - 1, s0)
                    hi = min(u0 + T + 1, s0 + slen)
                    if lo > u0 - 1:
                        MSET(K, H[:, :, 0:1], 0.0, [rH])
                    if hi < u0 + T + 1:
                        MSET(K, H[:, :, T + 1:T + 2], 0.0, [rH])
                    S.dma("sp", H[:, :, lo - (u0 - 1):hi - (u0 - 1)], hv[:, :, lo:hi], w=[rH])
                    S.dma("sp", X[:, :, :], xo[:, :, u0:u0 + T], w=[rX])
                    for cf in range(NFF):
                        psu, rpsu = psr.next()
                        for k in range(8):
                            MM(K, psu[:, 0:T], Wup[:, k, cf * 128:(cf + 1) * 128], H[:, k, 1:T + 1], k == 0, k == 7,
                               [rWup, rH], [rpsu])
                        psg, rpsg = psr.next()
                        for k in range(8):
                            MM(K, psg[:, 0:T + 2], Wup[:, k, DFF + cf * 128:DFF + (cf + 1) * 128], H[:, k, 0:T + 2], k == 0, k == 7,
                               [rWup, rH], [rpsg])
                        TS(K, C1[:], psg[:, 0:T], cw[:, l, 0, cf:cf + 1], cb[:, l, cf:cf + 1], ALU.mult, ALU.add,
                           [rpsg, rcw, rcb], [rC1])
                        STT(K, C2[:], psg[:, 1:T + 1], cw[:, l, 1, cf:cf + 1], C1[:], ALU.mult, ALU.add, [rpsg, rcw, rC1], [rC2])
                        STT(K, C3[:], psg[:, 2:T + 2], cw[:, l, 2, cf:cf + 1], C2[:], ALU.mult, ALU.add, [rpsg, rcw, rC2], [rC3])
                        ACT(K, SL[:], C3[:], AF.Silu, [rC3], [rSL])
                        TT(K, A_[:, cf, :], psu[:, 0:T], SL[:], ALU.mult, [rpsu, rSL], [rA])
                    for fc in range(8):
                        ps, rps = psr.next()
                        for cf in range(NFF):
                            MM(K, ps[:, 0:T], Wdn[:, cf, fc * 128:(fc + 1) * 128], A_[:, cf, :], cf == 0, cf == NFF - 1,
                               [rWdn, rA], [rps])
                        STT(K, X[:, fc, :], ps[:, 0:T], K.MOD[:, l, 5, fc, j:j + 1], X[:, fc, :], ALU.mult, ALU.add,
                            [rps, K.rMOD, rX], [rX])
                    S.dma("pool", xo[:, :, u0:u0 + T], X[:, :, :], r=[rX])


def phase_final(K):
    nc, S, I, SC = K.nc, K.S, K.I, K.SC
    T = 512
    with Phase(K, "Z") as P:
        nf, rnf = P.sb("nf", [128, 8], F32)
        S.dma("sp", nf[:], I["norm_fT"], w=[rnf])
        Xb = Rot(P.sbn("X", [128, 8, T], F32, 2))
        Ob = Rot(P.sbn("O", [128, 8, T], F32, 2))
        SQ, rSQ = P.sb("SQ", [128, 8, T], BF16)
        RS, rRS = P.sb("RS", [128, T], F32)
        psr = Rot(list(zip(K.PS, K.rPS)))
        src_x = K.x_src(DEPTH)
        for b in range(NB):
            xv = src_x[b].rearrange("(k p) t -> p k t", p=128)
            ov = K.OUT[b].rearrange("(k p) t -> p k t", p=128)
            for i in range(L // T):
                u0 = LC + i * T
                X, rX = Xb.next()
                O, rO = Ob.next()
                S.dma("sp", X[:], xv[:, :, u0:u0 + T], w=[rX])
                ACT(K, SQ[:], X[:], AF.Square, [rX], [rSQ])
                ps, rps = psr.next()
                for k in range(8):
                    MM(K, ps[:, 0:T], K.ONESB[:, :], SQ[:, k, :], k == 0, k == 7, [rSQ, K.rONESB], [rps])
                rstd_from_psum(K, RS[:], ps[:, 0:T], D, [rps], [rRS])
                for k in range(8):
                    STT(K, O[:, k, :], X[:, k, :], nf[:, k:k + 1], RS[:], ALU.mult, ALU.mult, [rX, rnf, rRS], [rO])
                S.dma("pool", ov[:, :, i * T:(i + 1) * T], O[:], r=[rO])


IN_SPECS = {
    "xT": ([NB, D, NT], F32), "cT": ([128, 8, 3], F32), "w_ada": ([DEPTH, D, 6 * D], F32),
    "b_adaT": ([128, DEPTH * 48], F32), "norm1T": ([128, DEPTH * 8], F32), "norm2T": ([128, DEPTH * 8], F32),
    "norm_fT": ([128, 8], F32), "w_in_ext": ([DEPTH, D, INW], F32), "q_normT": ([128, DEPTH * 2], F32),
    "kv_normT": ([128, DEPTH], F32), "w_uq_ext": ([DEPTH, 256, 1024], F32), "w_ukv_r": ([DEPTH, 128, 1024], F32),
    "w_gk": ([DEPTH, 2, 16, 128], F32), "b_gk": ([DEPTH, 2, 128], F32), "gla_normT": ([128, DEPTH], F32),
    "w_pool_bd": ([DEPTH, 2, 128, 128], F32), "pool_scaleT": ([128, DEPTH * 2], F32),
    "w_o": ([DEPTH, D, D], F32), "w_up": ([DEPTH, D, 2 * DFF], F32), "w_down": ([DEPTH, DFF, D], F32),
    "conv_wT": ([128, DEPTH, 3, NFF], F32), "conv_bT": ([128, DEPTH, NFF], F32),
    "cos_t": ([128, L], F32), "sin_t": ([128, L], F32), "tri": ([2, 128, 128], F32), "mask": ([2, 128, 512], F32),
    "headmask": ([128, 4], F32), "blkmask": ([128, 256], F32), "ones64": ([128, 128], F32),
    "invcnt": ([256, NT], F32),
}

SCRATCH = {
    "XT": ([NB, D, NT], F32), "KT": ([NB, 8, 96, NT], BF16), "V": ([NB, NT, 8, 128], BF16),
    "QT": ([NB, 8, 96, NT], BF16), "POOLT": ([NB, 256, NT], F32), "GKT": ([NB, 128, NT], BF16),
    "GKV": ([NB, NT, 384], BF16), "GQT": ([NB, 128, NT], BF16), "LRT": ([NB, 2, 16, NT], BF16),
    "OGT": ([NB, 256, NT], BF16), "YT": ([NB, D, NT], BF16), "H2T": ([NB, D, NT], BF16), "OF": ([NB, 256, NT], F32),
}


def build(n_layers=DEPTH, phases="APXDF", debug=()):
    nc = bass.Bass("TRN2", target_bir_lowering=False)
    K = K_()
    K.nc = nc
    K.I = {k: nc.dram_tensor(k, shp, dt, kind="ExternalInput").ap() for k, (shp, dt) in IN_SPECS.items()}
    K.SC = {}
    for k, (shp, dt) in SCRATCH.items():
        kind = "ExternalOutput" if k in debug else "Internal"
        K.SC[k] = nc.dram_tensor("sc_" + k, shp, dt, kind=kind).ap()
    K.OUT = nc.dram_tensor("outT", [NB, D, L], F32, kind="ExternalOutput").ap()
    K.S = S = Sched(nc)
    S.bar_src = K.I["tri"][0, 0:1, 0:64]
    K.x_src = lambda l: (K.I["xT"] if l == 0 else K.SC["XT"])
    import os
    K.gla_lvl = int(os.environ.get("GLA_LVL", "9"))
    K.gla_nt = int(os.environ.get("GLA_NT", "999"))
    K.gla_x = int(os.environ.get("GLA_X", "0"))
    with ExitStack() as st:
        K.PS, K.rPS = [], []
        for i in range(8):
            K.PS.append(st.enter_context(nc.psum_tensor("psb%d" % i, [128, 512], F32)))
            K.rPS.append(S.res("@ps%d" % i))
        K.MOD = st.enter_context(nc.sbuf_tensor("MOD", [128, DEPTH, 6, 8, 3], F32))
        K.rMOD = S.res("@MOD")
        K.ONESB = st.enter_context(nc.sbuf_tensor("ONESB", [128, 128], BF16))
        K.rONESB = S.res("@ONESB")
        K.eps_col = st.enter_context(nc.sbuf_tensor("epsc", [128, 1], F32))
        K.one_col = st.enter_context(nc.sbuf_tensor("onec", [128, 1], F32))
        rc = S.res("@cols")
        MSET(K, K.ONESB[:], 1.0, [K.rONESB])
        MSET(K, K.eps_col[:], EPS, [rc])
        MSET(K, K.one_col[:], 1.0, [rc])
        if "debug_mod" in debug:
            K.MODO = nc.dram_tensor("modo", [128, DEPTH * 6 * 8 * 3], F32, kind="ExternalOutput").ap()
        phase_mod(K)
        if "debug_mod" in debug:
            S.dma("sp", K.MODO, K.MOD[:].rearrange("p a b c d -> p (a b c d)"), r=[K.rMOD])
            S.barrier()
        for l in range(n_layers):
            if "A" in phases:
                phase_A(K, l)
            if "P" in phases:
                phase_pool(K, l)
            if "X" in phases:
                phase_mix(K, l)
            if "M" in phases:
                phase_mla(K, l)
            if "G" in phases:
                phase_gla(K, l)
            if "D" in phases:
                phase_D1(K, l)
            if "F" in phases:
                phase_D2(K, l)
        if n_layers == DEPTH:
            phase_final(K)
        K.stats = S.emit()
    return nc, K


def _colT(v, k):
    return np.ascontiguousarray(np.asarray(v, np.float32).reshape(k, 128).T)


def shared_inputs(inp):
    f = lambda a: np.ascontiguousarray(np.asarray(a, np.float32))
    o = {}
    o["w_ada"] = f(inp["w_ada"])
    o["b_adaT"] = np.concatenate([_colT(inp["b_ada"][l], 48) for l in range(DEPTH)], axis=1)
    o["norm1T"] = np.concatenate([_colT(inp["norm1"][l], 8) for l in range(DEPTH)], axis=1)
    o["norm2T"] = np.concatenate([_colT(inp["norm2"][l], 8) for l in range(DEPTH)], axis=1)
    o["norm_fT"] = _colT(inp["norm_f"], 8)
    perm = np.zeros(32, np.int64)
    sign = np.zeros(32, np.float32)
    for ax in range(2):
        for hf in range(2):
            for fq in range(8):
                i = ax * 16 + hf * 8 + fq
                perm[i] = ax * 16 + (1 - hf) * 8 + fq
                sign[i] = -1.0 if hf == 0 else 1.0
    w_in = f(inp["w_in"])
    kr = w_in[:, :, 128:160]
    o["w_in_ext"] = np.ascontiguousarray(np.concatenate([w_in, kr[:, :, perm]], axis=2))
    o["q_normT"] = np.concatenate([_colT(inp["q_norm"][l], 2) for l in range(DEPTH)], axis=1)
    o["kv_normT"] = np.concatenate([_colT(inp["kv_norm"][l], 1) for l in range(DEPTH)], axis=1)
    wuq = f(inp["w_uq"]).reshape(DEPTH, 256, 8, 96)
    nope = wuq[:, :, :, :64].reshape(DEPTH, 256, 512)
    rope = wuq[:, :, :, 64:]
    ropep = rope[:, :, :, perm]
    o["w_uq_ext"] = np.ascontiguousarray(np.concatenate([nope, rope.reshape(DEPTH, 256, 256), ropep.reshape(DEPTH, 256, 256)], axis=2))
    wukv = f(inp["w_ukv"]).reshape(DEPTH, 128, 8, 128)
    o["w_ukv_r"] = np.ascontiguousarray(np.concatenate([wukv[:, :, :, :64].reshape(DEPTH, 128, 512),
                                                        wukv[:, :, :, 64:].reshape(DEPTH, 128, 512)], axis=2))
    o["w_gk"] = np.ascontiguousarray(np.stack([f(inp["w_gk_f"]), f(inp["w_gk_b"])], axis=1))
    o["b_gk"] = np.ascontiguousarray(np.stack([f(inp["b_gk_f"]), f(inp["b_gk_b"])], axis=1))
    gn = f(inp["gla_norm"])
    o["gla_normT"] = np.ascontiguousarray(np.concatenate([gn, gn], axis=1).T)
    wp = f(inp["w_pool"])
    bd = np.zeros((DEPTH, 2, 128, 128), np.float32)
    for g in range(4):
        c, s = g // 2, (g % 2) * 64
        bd[:, c, s:s + 64, s:s + 64] = wp[:, g]
    o["w_pool_bd"] = bd
    o["pool_scaleT"] = np.concatenate([_colT(inp["pool_scale"][l], 2) for l in range(DEPTH)], axis=1)
    o["w_o"] = f(inp["w_o"])
    o["w_up"] = f(inp["w_up"])
    o["w_down"] = f(inp["w_down"])
    cw = f(inp["conv_w"]).reshape(DEPTH, 3, NFF, 128)
    o["conv_wT"] = np.ascontiguousarray(cw.transpose(3, 0, 1, 2))
    o["conv_bT"] = np.ascontiguousarray(f(inp["conv_b"]).reshape(DEPTH, NFF, 128).transpose(2, 0, 1))
    rows = L // 64
    row = np.repeat(np.arange(rows), 64).astype(np.float32)
    col = np.tile(np.arange(64), rows).astype(np.float32)
    inv = (np.float32(10000.0) ** (-np.arange(0, 16, 2, dtype=np.float32) / np.float32(16))).astype(np.float32)
    ar = row[:, None] * inv
    ac = col[:, None] * inv
    ang = np.concatenate([ar, ar, ac, ac], axis=-1).astype(np.float32)
    cosT = np.cos(ang).astype(np.float32).T
    sinT = (np.sin(ang).astype(np.float32) * sign[None, :]).T
    o["cos_t"] = np.ascontiguousarray(np.tile(cosT, (4, 1)))
    o["sin_t"] = np.ascontiguousarray(np.tile(sinT, (4, 1)))
    jj = np.arange(128)[:, None]
    ii = np.arange(128)[None, :]
    same = (jj // 64) == (ii // 64)
    tri_f = (same & (jj <= ii)).astype(np.float32)
    tri_b = (same & (jj >= ii)).astype(np.float32)
    o["tri"] = np.stack([tri_f, tri_b]) * np.float32(-1.0 / 16.0)
    o["mask"] = np.stack([np.tile(tri_f, (1, 4)), np.tile(tri_b, (1, 4))]).astype(np.float32)
    hm = np.zeros((128, 4), np.float32)
    for h in range(4):
        hm[h * 32:(h + 1) * 32, h] = 1.0
    o["headmask"] = hm
    blk = np.zeros((128, 256), np.float32)
    for h in range(4):
        blk[h * 32:(h + 1) * 32, h * 64:(h + 1) * 64] = 1.0
    o["blkmask"] = blk
    o64 = np.zeros((128, 128), np.float32)
    o64[0:64, 0:64] = 1.0
    o64[64:128, 64:128] = 1.0
    o["ones64"] = o64
    ivc = np.zeros((256, NT), np.float32)
    for g, w in enumerate((2, 4, 8, 16)):
        for (s0, sl) in ((0, LC), (LC, L)):
            t = np.arange(sl)
            lo = np.clip(t - w // 2, 0, sl)
            hi = np.clip(t - w // 2 + w, 0, sl)
            ivc[g * 64:(g + 1) * 64, s0:s0 + sl] = (1.0 / (hi - lo).astype(np.float32))[None, :]
    o["invcnt"] = ivc
    return o


def core_inputs(inp, core, shared):
    b0 = core * NB
    x = np.asarray(inp["x"][b0:b0 + NB], np.float32)
    cx = np.asarray(inp["ctx"][b0:b0 + NB], np.float32)
    xT = np.ascontiguousarray(np.concatenate([cx, x], axis=1).transpose(0, 2, 1))
    vecs = np.stack([np.asarray(inp["c"][b0], np.float32), np.asarray(inp["c"][b0 + 1], np.float32),
                     np.asarray(inp["c_ctx"], np.float32)], axis=1)
    cT = np.ascontiguousarray(vecs.reshape(8, 128, 3).transpose(1, 0, 2))
    d = dict(shared)
    d["xT"] = xT
    d["cT"] = cT
    return d


_CACHE = {}


def kernel(**inputs):
    if "nc" not in _CACHE:
        _CACHE["nc"] = build()[0]
    nc = _CACHE["nc"]
    shared = shared_inputs(inputs)
    n = 8
    in_maps = [core_inputs(inputs, c, shared) for c in range(n)]
    res = run_bass_kernel_spmd(nc, in_maps, core_ids=list(range(n)))
    outs = [np.asarray(r["outT"]).transpose(0, 2, 1) for r in res.results]
    return np.ascontiguousarray(np.concatenate(outs, axis=0).astype(np.float32))
```

```python
import numpy as np
from contextlib import ExitStack
import concourse.bass as bass
import concourse.mybir as mybir
from concourse.bass_utils import run_bass_kernel_spmd

F32 = mybir.dt.float32
BF16 = mybir.dt.bfloat16
AF = mybir.ActivationFunctionType
ALU = mybir.AluOpType

ENGS = ("pe", "act", "dve", "pool", "sp")

NB = 2
D = 1024
L = 4096
LC = 256
NT = L + LC
DEPTH = 4
DFF = 2816
NFF = 22
EPS = 1e-6
INW = 1504


class Res:
    __slots__ = ("name", "w", "r")

    def __init__(self, name):
        self.name = name
        self.w = []
        self.r = []


class Sched:
    SAME_ENGINE_SYNC = False

    def __init__(self, nc, n_dma_sems=48):
        self.nc = nc
        self.q = {e: [] for e in ENGS}
        self.esem = {e: nc.alloc_semaphore("S_" + e) for e in ENGS if e != "sp"}
        self.dsems = [nc.alloc_semaphore("D%d" % i) for i in range(n_dma_sems)]
        self.dcount = [0] * n_dma_sems
        self.dq = {"sp": list(range(0, 28)), "pool": list(range(28, 44)), "act": list(range(44, 48))}
        self.dqi = {"sp": 0, "pool": 0, "act": 0}
        self.bar_sem = nc.alloc_semaphore("BAR")
        self.bar_count = 0
        self.bar_scratch = nc.dram_tensor("bar_scratch", [2, 64], F32).ap()
        self.all_res = []

    def res(self, name):
        r = Res(name)
        self.all_res.append(r)
        return r

    def _deps(self, eng, reads, writes, is_dma=False):
        deps = []
        for r in reads:
            deps.extend(r.w)
        for w in writes:
            if is_dma:
                deps.extend(t for t in w.w if t[0] != "d")
            else:
                deps.extend(w.w)
            deps.extend(w.r)
        cmax = {}
        dset = {}
        for d in deps:
            if d[0] == "c":
                if d[1] == eng and (eng == "pe" or not self.SAME_ENGINE_SYNC):
                    continue
                if cmax.get(d[1], -1) < d[2]:
                    cmax[d[1]] = d[2]
            else:
                if dset.get(d[1], 0) < d[2]:
                    dset[d[1]] = d[2]
        return [("c", e, i) for e, i in cmax.items()] + [("d", si, v) for si, v in dset.items()]

    def op(self, eng, fn, r=(), w=()):
        deps = self._deps(eng, r, w)
        idx = len(self.q[eng])
        import sys as _s
        fr = _s._getframe(2)
        self.q[eng].append({"fn": fn, "deps": deps, "kind": "c", "tag": "L%d" % fr.f_lineno})
        tok = ("c", eng, idx)
        for x in r:
            x.r.append(tok)
        for x in w:
            x.w = [tok]
            x.r = []
        return tok

    def dma(self, queue, out, in_, r=(), w=()):
        deps = self._deps(queue, r, w, is_dma=True)
        lst = self.dq[queue]
        si = lst[self.dqi[queue] % len(lst)]
        self.dqi[queue] += 1
        if self.dcount[si] > 0:
            deps = [d for d in deps if not (d[0] == "d" and d[1] == si)] + [("d", si, self.dcount[si])]
        self.dcount[si] += 16
        val = self.dcount[si]
        self.q[queue].append({"fn": (lambda e: e.dma_start(out=out, in_=in_)), "deps": deps, "kind": "d", "sem": si})
        tok = ("d", si, val)
        for x in r:
            x.r.append(tok)
        for x in w:
            x.w = [t for t in x.w if t[0] == "d"][-7:] + [tok]
            x.r = []
        return tok

    def barrier(self):
        deps = []
        for e in ENGS:
            if e == "sp":
                continue
            for j in range(len(self.q[e]) - 1, -1, -1):
                if self.q[e][j]["kind"] == "c":
                    deps.append(("c", e, j))
                    break
        for si in range(len(self.dsems)):
            if self.dcount[si] > 0:
                deps.append(("d", si, self.dcount[si]))
        self.bar_count += 16
        bc = self.bar_count
        sc = self.bar_scratch
        src = self.bar_src
        self.q["sp"].append({"fn": (lambda e: e.dma_start(out=sc[1:2, :], in_=src)), "deps": deps, "kind": "b"})
        for e in ENGS:
            self.q[e].append({"fn": None, "deps": [("b", bc)], "kind": "w"})
        for r in self.all_res:
            r.w = []
            r.r = []
        self.all_res = [r for r in self.all_res if getattr(r, "name", "").startswith("@")]

    def emit(self):
        nc = self.nc
        targets = {e: set() for e in ENGS}
        for e in ENGS:
            for o in self.q[e]:
                for d in o["deps"]:
                    if d[0] == "c":
                        targets[d[1]].add(d[2])
        rank = {}
        for e in ENGS:
            for i, idx in enumerate(sorted(targets[e])):
                rank[(e, idx)] = i + 1
        stats = {}

        def run(eng_name, e):
            waited = {}
            nw = 0
            if not hasattr(self, "_log"):
                self._log = {}
            lg = self._log.setdefault(eng_name, [])
            for idx, o in enumerate(self.q[eng_name]):
                for d in o["deps"]:
                    if d[0] == "c":
                        sem, val, key = self.esem[d[1]], rank[(d[1], d[2])], "c" + d[1]
                    elif d[0] == "d":
                        sem, val, key = self.dsems[d[1]], d[2], "d%d" % d[1]
                    else:
                        sem, val, key = self.bar_sem, d[1], "bar"
                    if waited.get(key, 0) >= val:
                        continue
                    waited[key] = val
                    e.wait_ge(sem, val)
                    lg.append("   wait %s >= %d" % (key, val))
                    nw += 1
                if o["fn"] is None:
                    continue
                ins = o["fn"](e)
                lg.append("%d %s %s inc=%s" % (idx, o["kind"], o.get("tag", ""), (eng_name, idx) in rank))
                if o["kind"] == "c":
                    if (eng_name, idx) in rank:
                        ins.then_inc(self.esem[eng_name], 1)
                elif o["kind"] == "d":
                    ins.then_inc(self.dsems[o["sem"]], 16)
                elif o["kind"] == "b":
                    ins.then_inc(self.bar_sem, 16)
            stats[eng_name] = (len(self.q[eng_name]), nw)
            import os
            if os.environ.get("DUMP"):
                with open(os.environ["DUMP"] + "_" + eng_name + ".txt", "w") as f:
                    for l_ in self._log[eng_name]:
                        f.write(l_ + "\n")

        with nc.Block() as block:
            @block.sync
            def _(e):
                run("sp", e)

            @block.tensor
            def _(e):
                run("pe", e)

            @block.scalar
            def _(e):
                run("act", e)

            @block.vector
            def _(e):
                run("dve", e)

            @block.gpsimd
            def _(e):
                run("pool", e)
        return stats


class K_:
    pass


def MM(K, out, lhsT, rhs, start, stop, r, w, skip=False):
    if skip:
        K.S.op("pe", lambda e: e.matmul(out, lhsT=lhsT, rhs=rhs, start=start, stop=stop, skip_group_check=True), r, w)
    else:
        K.S.op("pe", lambda e: e.matmul(out, lhsT=lhsT, rhs=rhs, start=start, stop=stop), r, w)


def ACT(K, out, in_, func, r, w, scale=None, bias=None):
    kw = {}
    if scale is not None:
        kw["scale"] = scale
    if bias is not None:
        kw["bias"] = bias
    K.S.op("act", lambda e: e.activation(out=out, in_=in_, func=func, **kw), r, w)


def TT(K, out, in0, in1, op, r, w, eng="dve"):
    K.S.op(eng, lambda e: e.tensor_tensor(out=out, in0=in0, in1=in1, op=op), r, w)


def TS(K, out, in0, s1, s2, op0, op1, r, w, eng="dve"):
    if op1 is None:
        K.S.op(eng, lambda e: e.tensor_scalar(out=out, in0=in0, scalar1=s1, scalar2=None, op0=op0), r, w)
    else:
        K.S.op(eng, lambda e: e.tensor_scalar(out=out, in0=in0, scalar1=s1, scalar2=s2, op0=op0, op1=op1), r, w)


def STT(K, out, in0, scalar, in1, op0, op1, r, w):
    K.S.op("dve", lambda e: e.scalar_tensor_tensor(out=out, in0=in0, scalar=scalar, in1=in1, op0=op0, op1=op1), r, w)


def CP(K, out, in_, r, w, eng="dve"):
    if eng == "act":
        K.S.op("act", lambda e: e.copy(out=out, in_=in_), r, w)
    else:
        K.S.op(eng, lambda e: e.tensor_copy(out=out, in_=in_), r, w)


def MSET(K, ap, val, w, eng="dve"):
    K.S.op(eng, lambda e: e.memset(ap, val), (), w)


def RECIP(K, out, in_, r, w):
    K.S.op("dve", lambda e: e.reciprocal(out=out, in_=in_), r, w)


class Phase:
    def __init__(self, K, name):
        self.K = K
        self.name = name
        self.st = ExitStack()
        self.n = 0

    def __enter__(self):
        self.st.__enter__()
        return self

    def __exit__(self, *a):
        self.K.S.barrier()
        return self.st.__exit__(*a)

    def sb(self, name, shape, dt):
        self.n += 1
        nm = "%s_%s_%d" % (self.name, name, self.n)
        t = self.st.enter_context(self.K.nc.sbuf_tensor(nm, shape, dt))
        return t, self.K.S.res(nm)

    def sbn(self, name, shape, dt, n):
        return [self.sb(name + str(i), shape, dt) for i in range(n)]


class Rot:
    def __init__(self, items):
        self.items = items
        self.i = 0

    def next(self):
        x = self.items[self.i % len(self.items)]
        self.i += 1
        return x


def load_w(K, P, dst, rdst, src, rows, cols, three_d=True, eng="pool"):
    if not hasattr(P, "_stg"):
        P._stg = Rot(P.sbn("stg", [128, 1024], F32, 3))
    nk = max(1, rows // 128)
    for kc in range(nk):
        rp = min(128, rows)
        for c0 in range(0, cols, 1024):
            wd = min(1024, cols - c0)
            stg, rstg = P._stg.next()
            K.S.dma("sp", stg[0:rp, 0:wd], src[kc * 128:kc * 128 + rp, c0:c0 + wd], w=[rstg])
            if three_d:
                o = dst[0:rp, kc, c0:c0 + wd]
            else:
                o = dst[0:rp, c0:c0 + wd]
            CP(K, o, stg[0:rp, 0:wd], [rstg], [rdst], eng=eng)


def tiles_of_batch():
    t = [(0, LC, True, 0)]
    for i in range(L // 512):
        t.append((LC + i * 512, 512, False, i * 512))
    return t


def rstd_from_psum(K, out, ps_ap, n, r, w):
    ACT(K, out, ps_ap, AF.Ln, r, w, scale=1.0 / n, bias=K.eps_col[:, 0:1])
    ACT(K, out, out, AF.Exp, w, w, scale=-0.5)


def phase_mod(K):
    nc, S, I = K.nc, K.S, K.I
    with Phase(K, "mod") as P:
        cT, rcT = P.sb("cT", [128, 8, 3], F32)
        sc, rsc = P.sb("sc", [128, 8, 3], F32)
        bad, rbad = P.sb("bad", [128, DEPTH * 48], F32)
        S.dma("sp", cT[:], I["cT"], w=[rcT])
        S.dma("sp", bad[:], I["b_adaT"], w=[rbad])
        ACT(K, sc[:], cT[:], AF.Silu, [rcT], [rsc])
        wb = Rot(P.sbn("wada", [128, 8, 512], F32, 2))
        psr = Rot(list(zip(K.PS, K.rPS)))
        for l in range(DEPTH):
            for cb in range(12):
                W, rW = wb.next()
                src = I["w_ada"][l].rearrange("(k p) c -> p k c", p=128)[:, :, cb * 512:(cb + 1) * 512]
                S.dma("sp", W[:], src, w=[rW])
                ps, rps = psr.next()
                for f in range(4):
                    for k in range(8):
                        MM(K, ps[:, f * 4:f * 4 + 3], W[:, k, f * 128:(f + 1) * 128], sc[:, k, :], k == 0, k == 7,
                           [rW, rsc], [rps])
                for f in range(4):
                    cidx = cb * 4 + f
                    m, kk = cidx // 8, cidx % 8
                    TS(K, K.MOD[:, l, m, kk, :], ps[:, f * 4:f * 4 + 3], bad[:, l * 48 + cidx:l * 48 + cidx + 1], None,
                       ALU.add, None, [rps, rbad], [K.rMOD])
        n1, rn1 = P.sb("n1", [128, DEPTH * 8], F32)
        n2, rn2 = P.sb("n2", [128, DEPTH * 8], F32)
        S.dma("sp", n1[:], I["norm1T"], w=[rn1])
        S.dma("sp", n2[:], I["norm2T"], w=[rn2])
        for l in range(DEPTH):
            for k in range(8):
                for (m, nn, rnn) in ((1, n1, rn1), (4, n2, rn2)):
                    TS(K, K.MOD[:, l, m, k, :], K.MOD[:, l, m, k, :], 1.0, nn[:, l * 8 + k:l * 8 + k + 1], ALU.add, ALU.mult,
                       [K.rMOD, rnn], [K.rMOD])


def phase_A(K, l):
    nc, S, I, SC = K.nc, K.S, K.I, K.SC
    with Phase(K, "A%d" % l) as P:
        Win, rWin = P.sb("Win", [128, 8, INW], BF16)
        Wuq, rWuq = P.sb("Wuq", [128, 2, 1024], BF16)
        Wukv, rWukv = P.sb("Wukv", [128, 1024], BF16)
        load_w(K, P, Win, rWin, I["w_in_ext"][l], 1024, INW)
        load_w(K, P, Wuq, rWuq, I["w_uq_ext"][l], 256, 1024)
        load_w(K, P, Wukv, rWukv, I["w_ukv_r"][l], 128, 1024, three_d=False)
        COS, rCOS = P.sb("cos", [128, L], F32)
        SIN, rSIN = P.sb("sin", [128, L], F32)
        S.dma("sp", COS[:], I["cos_t"], w=[rCOS])
        S.dma("sp", SIN[:], I["sin_t"], w=[rSIN])
        qn, rqn = P.sb("qn", [128, DEPTH * 2], F32)
        kvn, rkvn = P.sb("kvn", [128, DEPTH], F32)
        S.dma("sp", qn[:], I["q_normT"], w=[rqn])
        S.dma("sp", kvn[:], I["kv_normT"], w=[rkvn])
        Xb = Rot(P.sbn("X", [128, 8, 512], F32, 2))
        SQ, rSQ = P.sb("SQ", [128, 8, 512], BF16)
        RS, rRS = P.sb("RS", [128, 512], F32)
        TMP, rTMP = P.sb("TMP", [128, 512], F32)
        H, rH = P.sb("H", [128, 8, 512], BF16)
        SQ2, rSQ2 = P.sb("SQ2", [128, 2, 512], BF16)
        RS2, rRS2 = P.sb("RS2", [128, 512], F32)
        CKN, rCKN = P.sb("CKN", [128, 512], BF16)
        CQN, rCQN = P.sb("CQN", [128, 2, 512], BF16)
        VT, rVT = P.sb("VT", [128, 4, 8, 128], BF16)
        MSET(K, VT[:], 1.0, [rVT])
        GKVt, rGKVt = P.sb("GKVt", [128, 4, 384], BF16)
        T1, rT1 = P.sb("T1", [128, 512], F32)
        T2, rT2 = P.sb("T2", [128, 512], F32)
        ob = Rot(P.sbn("OB", [128, 512], BF16, 6))
        of = Rot(P.sbn("OF", [128, 512], F32, 3))
        psr = Rot(list(zip(K.PS, K.rPS)))
        src_x = K.x_src(l)
        for b in range(NB):
            xv = src_x[b].rearrange("(k p) t -> p k t", p=128)
            for (u0, T, isc, pos0) in tiles_of_batch():
                j = 2 if isc else b
                nblk = T // 128
                X, rX = Xb.next()
                S.dma("sp", X[:, :, 0:T], xv[:, :, u0:u0 + T], w=[rX])
                ACT(K, SQ[:, :, 0:T], X[:, :, 0:T], AF.Square, [rX], [rSQ])
                ps, rps = psr.next()
                for k in range(8):
                    MM(K, ps[:, 0:T], K.ONESB[:, :], SQ[:, k, 0:T], k == 0, k == 7, [rSQ, K.rONESB], [rps])
                rstd_from_psum(K, RS[:, 0:T], ps[:, 0:T], D, [rps], [rRS])
                for k in range(8):
                    STT(K, TMP[:, 0:T], X[:, k, 0:T], K.MOD[:, l, 1, k, j:j + 1], RS[:, 0:T], ALU.mult, ALU.mult,
                        [rX, rRS, K.rMOD], [rTMP])
                    ACT(K, H[:, k, 0:T], TMP[:, 0:T], AF.Identity, [rTMP, K.rMOD], [rH], bias=K.MOD[:, l, 0, k, j:j + 1])

                def proj(c0, m, ps_ap, rps_):
                    for k in range(8):
                        MM(K, ps_ap, Win[:, k, c0:c0 + m], H[:, k, 0:T], k == 0, k == 7, [rWin, rH], [rps_])

                ps, rps = psr.next()
                proj(0, 128, ps[:, 0:T], rps)
                ACT(K, SQ2[:, 0, 0:T], ps[:, 0:T], AF.Square, [rps], [rSQ2])
                ps2, rps2 = psr.next()
                MM(K, ps2[:, 0:T], K.ONESB[:, :], SQ2[:, 0, 0:T], True, True, [rSQ2, K.rONESB], [rps2])
                rstd_from_psum(K, RS2[:, 0:T], ps2[:, 0:T], 128, [rps2], [rRS2])
                STT(K, CKN[:, 0:T], ps[:, 0:T], kvn[:, l:l + 1], RS2[:, 0:T], ALU.mult, ALU.mult, [rps, rkvn, rRS2], [rCKN])
                for hp in range(4):
                    ps, rps = psr.next()
                    MM(K, ps[:, 0:T], Wukv[:, hp * 128:(hp + 1) * 128], CKN[:, 0:T], True, True, [rWukv, rCKN], [rps])
                    o, ro = ob.next()
                    CP(K, o[:, 0:T], ps[:, 0:T], [rps], [ro], eng="act")
                    for hh in range(2):
                        S.dma("pool", SC["KT"][b, hp * 2 + hh, 0:64, u0:u0 + T], o[hh * 64:(hh + 1) * 64, 0:T], r=[ro])
                for tb in range(nblk):
                    ps, rps = psr.next()
                    MM(K, ps[:, 0:512], CKN[:, tb * 128:(tb + 1) * 128], Wukv[:, 512:1024], True, True, [rWukv, rCKN], [rps])
                    CP(K, VT[:, tb, :, 0:64], ps[:, 0:512].rearrange("p (h c) -> p h c", c=64), [rps], [rVT])
                S.dma("pool", SC["V"][b, u0:u0 + T].rearrange("(n p) h c -> p n (h c)", p=128), VT[:, 0:nblk].rearrange("p n h c -> p n (h c)"), r=[rVT])
                ps, rps = psr.next()
                proj(128, 32, ps[0:32, 0:T], rps)
                o, ro = ob.next()
                if isc:
                    CP(K, o[0:32, 0:T], ps[0:32, 0:T], [rps], [ro], eng="act")
                else:
                    ps2, rps2 = psr.next()
                    proj(1472, 32, ps2[0:32, 0:T], rps2)
                    TT(K, T1[0:32, 0:T], ps[0:32, 0:T], COS[0:32, pos0:pos0 + T], ALU.mult, [rps, rCOS], [rT1])
                    TT(K, T2[0:32, 0:T], ps2[0:32, 0:T], SIN[0:32, pos0:pos0 + T], ALU.mult, [rps2, rSIN], [rT2])
                    TT(K, o[0:32, 0:T], T1[0:32, 0:T], T2[0:32, 0:T], ALU.add, [rT1, rT2], [ro])
                for h in range(8):
                    S.dma("pool", SC["KT"][b, h, 64:96, u0:u0 + T], o[0:32, 0:T], r=[ro])
                ps, rps = psr.next()
                proj(160, 128, ps[:, 0:T], rps)
                o, ro = ob.next()
                CP(K, o[:, 0:T], ps[:, 0:T], [rps], [ro], eng="act")
                S.dma("pool", SC["GKT"][b, :, u0:u0 + T], o[:, 0:T], r=[ro])
                for tb in range(nblk):
                    ps, rps = psr.next()
                    for k in range(8):
                        MM(K, ps[:, 0:384], H[:, k, tb * 128:(tb + 1) * 128], Win[:, k, 160:544], k == 0, k == 7,
                           [rWin, rH], [rps])
                    CP(K, GKVt[:, tb, :], ps[:, 0:384], [rps], [rGKVt])
                S.dma("pool", SC["GKV"][b, u0:u0 + T].rearrange("(n p) c -> p n c", p=128), GKVt[:, 0:nblk], r=[rGKVt])
                ps, rps = psr.next()
                proj(544, 32, ps[0:32, 0:T], rps)
                o, ro = ob.next()
                CP(K, o[0:32, 0:T], ps[0:32, 0:T], [rps], [ro], eng="act")
                S.dma("pool", SC["LRT"][b, 0, :, u0:u0 + T], o[0:16, 0:T], r=[ro])
                S.dma("pool", SC["LRT"][b, 1, :, u0:u0 + T], o[16:32, 0:T], r=[ro])
                for c in range(2):
                    ps, rps = psr.next()
                    proj(576 + c * 128, 128, ps[:, 0:T], rps)
                    o, ro = of.next()
                    CP(K, o[:, 0:T], ps[:, 0:T], [rps], [ro], eng="act")
                    S.dma("pool", SC["POOLT"][b, c * 128:(c + 1) * 128, u0:u0 + T], o[:, 0:T], r=[ro])
                psa, rpsa = psr.next()
                proj(832, 128, psa[:, 0:T], rpsa)
                psb, rpsb = psr.next()
                proj(960, 128, psb[:, 0:T], rpsb)
                ACT(K, SQ2[:, 0, 0:T], psa[:, 0:T], AF.Square, [rpsa], [rSQ2])
                ACT(K, SQ2[:, 1, 0:T], psb[:, 0:T], AF.Square, [rpsb], [rSQ2])
                ps2, rps2 = psr.next()
                for k in range(2):
                    MM(K, ps2[:, 0:T], K.ONESB[:, :], SQ2[:, k, 0:T], k == 0, k == 1, [rSQ2, K.rONESB], [rps2])
                rstd_from_psum(K, RS2[:, 0:T], ps2[:, 0:T], 256, [rps2], [rRS2])
                STT(K, CQN[:, 0, 0:T], psa[:, 0:T], qn[:, l * 2:l * 2 + 1], RS2[:, 0:T], ALU.mult, ALU.mult,
                    [rpsa, rqn, rRS2], [rCQN])
                STT(K, CQN[:, 1, 0:T], psb[:, 0:T], qn[:, l * 2 + 1:l * 2 + 2], RS2[:, 0:T], ALU.mult, ALU.mult,
                    [rpsb, rqn, rRS2], [rCQN])
                for hp in range(4):
                    ps, rps = psr.next()
                    for k in range(2):
                        MM(K, ps[:, 0:T], Wuq[:, k, hp * 128:(hp + 1) * 128], CQN[:, k, 0:T], k == 0, k == 1, [rWuq, rCQN], [rps])
                    o, ro = ob.next()
                    CP(K, o[:, 0:T], ps[:, 0:T], [rps], [ro], eng="act")
                    for hh in range(2):
                        S.dma("pool", SC["QT"][b, hp * 2 + hh, 0:64, u0:u0 + T], o[hh * 64:(hh + 1) * 64, 0:T], r=[ro])
                for g in range(2):
                    ps, rps = psr.next()
                    for k in range(2):
                        MM(K, ps[:, 0:T], Wuq[:, k, 512 + g * 128:512 + (g + 1) * 128], CQN[:, k, 0:T], k == 0, k == 1,
                           [rWuq, rCQN], [rps])
                    o, ro = ob.next()
                    if isc:
                        CP(K, o[:, 0:T], ps[:, 0:T], [rps], [ro], eng="act")
                    else:
                        ps2, rps2 = psr.next()
                        for k in range(2):
                            MM(K, ps2[:, 0:T], Wuq[:, k, 768 + g * 128:768 + (g + 1) * 128], CQN[:, k, 0:T], k == 0, k == 1,
                               [rWuq, rCQN], [rps2])
                        TT(K, T1[:, 0:T], ps[:, 0:T], COS[:, pos0:pos0 + T], ALU.mult, [rps, rCOS], [rT1])
                        TT(K, T2[:, 0:T], ps2[:, 0:T], SIN[:, pos0:pos0 + T], ALU.mult, [rps2, rSIN], [rT2])
                        TT(K, o[:, 0:T], T1[:, 0:T], T2[:, 0:T], ALU.add, [rT1, rT2], [ro])
                    for hh in range(4):
                        S.dma("pool", SC["QT"][b, g * 4 + hh, 64:96, u0:u0 + T], o[hh * 32:(hh + 1) * 32, 0:T], r=[ro])
                ps, rps = psr.next()
                proj(1088, 128, ps[:, 0:T], rps)
                o, ro = ob.next()
                CP(K, o[:, 0:T], ps[:, 0:T], [rps], [ro], eng="act")
                S.dma("pool", SC["GQT"][b, :, u0:u0 + T], o[:, 0:T], r=[ro])
                for c in range(2):
                    ps, rps = psr.next()
                    proj(1216 + c * 128, 128, ps[:, 0:T], rps)
                    o, ro = ob.next()
                    CP(K, o[:, 0:T], ps[:, 0:T], [rps], [ro], eng="act")
                    S.dma("pool", SC["OGT"][b, c * 128:(c + 1) * 128, u0:u0 + T], o[:, 0:T], r=[ro])


def phase_pool(K, l):
    nc, S, I, SC = K.nc, K.S, K.I, K.SC
    with Phase(K, "P%d" % l) as P:
        WP, rWP = P.sb("WP", [128, 2, 128], BF16)
        for c in range(2):
            load_w(K, P, WP[:, c, :], rWP, I["w_pool_bd"][l, c], 128, 128, three_d=False)
        psc, rpsc = P.sb("psc", [128, DEPTH * 2], F32)
        S.dma("sp", psc[:], I["pool_scaleT"], w=[rpsc])
        Ub = Rot(P.sbn("U", [128, 2, 528], F32, 2))
        IVb = Rot(P.sbn("IV", [128, 2, 512], F32, 2))
        A_, rA = P.sb("A", [128, 528], F32)
        B_, rB = P.sb("B", [128, 528], F32)
        C_, rC = P.sb("C", [128, 528], F32)
        E_, rE = P.sb("E", [128, 528], F32)
        Mn, rMn = P.sb("Mn", [128, 512], F32)
        Df, rDf = P.sb("Df", [128, 2, 512], BF16)
        yb = Rot(P.sbn("Y", [128, 512], BF16, 3))
        psr = Rot(list(zip(K.PS, K.rPS)))
        for b in range(NB):
            for (u0, T, isc, pos0) in tiles_of_batch():
                s0, slen = (0, LC) if isc else (LC, L)
                U, rU = Ub.next()
                IV, rIV = IVb.next()
                lo = max(u0 - 8, s0)
                hi = min(u0 + T + 8, s0 + slen)
                if lo > u0 - 8:
                    MSET(K, U[:, :, 0:8], 0.0, [rU])
                if hi < u0 + T + 8:
                    MSET(K, U[:, :, T + 8:T + 16], 0.0, [rU])
                S.dma("sp", U[:, :, lo - (u0 - 8):hi - (u0 - 8)],
                      SC["POOLT"][b].rearrange("(c p) t -> p c t", p=128)[:, :, lo:hi], w=[rU])
                S.dma("sp", IV[:, :, 0:T], I["invcnt"].rearrange("(c p) t -> p c t", p=128)[:, :, u0:u0 + T], w=[rIV])
                W = T + 16
                for c in range(2):
                    u = U[:, c, :]
                    TT(K, A_[:, 1:W], u[:, 0:W - 1], u[:, 1:W], ALU.add, [rU], [rA])
                    if c == 0:
                        TT(K, B_[64:128, 2:W - 1], A_[64:128, 1:W - 2], A_[64:128, 3:W], ALU.add, [rA], [rB])
                        TT(K, Mn[0:64, 0:T], A_[0:64, 8:8 + T], IV[0:64, c, 0:T], ALU.mult, [rA, rIV], [rMn])
                        TT(K, Mn[64:128, 0:T], B_[64:128, 8:8 + T], IV[64:128, c, 0:T], ALU.mult, [rB, rIV], [rMn])
                    else:
                        TT(K, B_[:, 2:W - 1], A_[:, 1:W - 2], A_[:, 3:W], ALU.add, [rA], [rB])
                        TT(K, C_[:, 4:W - 3], B_[:, 2:W - 5], B_[:, 6:W - 1], ALU.add, [rB], [rC])
                        TT(K, E_[64:128, 8:W - 7], C_[64:128, 4:W - 11], C_[64:128, 12:W - 3], ALU.add, [rC], [rE])
                        TT(K, Mn[0:64, 0:T], C_[0:64, 8:8 + T], IV[0:64, c, 0:T], ALU.mult, [rC, rIV], [rMn])
                        TT(K, Mn[64:128, 0:T], E_[64:128, 8:8 + T], IV[64:128, c, 0:T], ALU.mult, [rE, rIV], [rMn])
                    TT(K, Df[:, c, 0:T], Mn[:, 0:T], u[:, 8:8 + T], ALU.subtract, [rMn, rU], [rDf])
                    ps, rps = psr.next()
                    MM(K, ps[:, 0:T], WP[:, c, :], Df[:, c, 0:T], True, True, [rWP, rDf], [rps])
                    Y, rY = yb.next()
                    ACT(K, Y[:, 0:T], ps[:, 0:T], AF.Identity, [rps, rpsc], [rY], scale=psc[:, l * 2 + c:l * 2 + c + 1])
                    S.dma("pool", SC["YT"][b, c * 128:(c + 1) * 128, u0:u0 + T], Y[:, 0:T], r=[rY])


def phase_mla(K, l):
    nc, S, I, SC = K.nc, K.S, K.I, K.SC
    scale = float(96 ** -0.5)
    NKC = NT // 128
    with Phase(K, "M%d" % l) as P:
        VB, rVB = P.sb("VB", [128, NKC, 8, 128], BF16)
        KTb = Rot(P.sbn("KT", [96, NT], BF16, 2))
        QTb = Rot(P.sbn("QT", [96, 512], BF16, 2))
        PTb = Rot(P.sbn("PT", [128, 512], BF16, 4))
        R_, rR = P.sb("R", [64, 512], F32)
        Yb = Rot(P.sbn("Y", [64, 512], BF16, 2))
        obank = Rot([(K.PS[i], K.rPS[i]) for i in (0, 1)])
        sbank = Rot([(K.PS[i], K.rPS[i]) for i in (2, 3, 4, 5, 6, 7)])
        for b in range(NB):
            vsrc = SC["V"][b].rearrange("(n p) h c -> p n (h c)", p=128)
            vdst = VB[:].rearrange("p n h c -> p n (h c)")
            for n0 in range(0, NKC, 6):
                n1 = min(NKC, n0 + 6)
                S.dma("sp", vdst[:, n0:n1], vsrc[:, n0:n1], w=[rVB])
            for h in range(8):
                KT, rKT = KTb.next()
                S.dma("sp", KT[:, :], SC["KT"][b, h], w=[rKT])
                for (u0, T, isc, pos0) in tiles_of_batch():
                    nk = (LC // 128) if isc else NKC
                    QT, rQT = QTb.next()
                    S.dma("sp", QT[:, 0:T], SC["QT"][b, h, :, u0:u0 + T], w=[rQT])
                    O, rO = obank.next()
                    pend = None
                    sps = {}

                    def smm(kc):
                        ps, rps = sbank.next()
                        MM(K, ps[:, 0:T], KT[:, kc * 128:(kc + 1) * 128], QT[:, 0:T], True, True, [rKT, rQT], [rps])
                        sps[kc] = (ps, rps)

                    smm(0)
                    for kc in range(nk):
                        if kc + 1 < nk:
                            smm(kc + 1)
                        ps, rps = sps.pop(kc)
                        PT, rPT = PTb.next()
                        ACT(K, PT[:, 0:T], ps[:, 0:T], AF.Exp, [rps], [rPT], scale=scale)
                        MM(K, O[:, 0:T], VB[:, kc, h, :], PT[:, 0:T], kc == 0, kc == nk - 1, [rVB, rPT], [rO])
                    CP(K, R_[0:64, 0:T], O[64:128, 0:T], [rO], [rR], eng="act")
                    RECIP(K, R_[0:64, 0:T], R_[0:64, 0:T], [rR], [rR])
                    Y, rY = Yb.next()
                    TT(K, Y[0:64, 0:T], O[0:64, 0:T], R_[0:64, 0:T], ALU.mult, [rO, rR], [rY])
                    S.dma("pool", SC["YT"][b, 256 + h * 64:256 + (h + 1) * 64, u0:u0 + T], Y[0:64, 0:T], r=[rY])


def phase_gla(K, l):
    nc, S, I, SC = K.nc, K.S, K.I, K.SC
    qs = float(32 ** -0.5)
    with Phase(K, "G%d" % l) as P:
        Wgk, rWgk = P.sb("Wgk", [128, 2, 128], BF16)
        MSET(K, Wgk[:], 0.0, [rWgk])
        for d in range(2):
            load_w(K, P, Wgk[:, d, :], rWgk, I["w_gk"][l, d], 16, 128, three_d=False)
        BG, rBG = P.sb("BG", [128, 2, 128], F32)
        for d in range(2):
            S.dma("sp", BG[:, d, :], I["b_gk"][l, d:d + 1, :].partition_broadcast(128), w=[rBG])
        TRI, rTRI = P.sb("TRI", [128, 2, 128], F32)
        MSK, rMSK = P.sb("MSK", [128, 2, 512], F32)
        for d in range(2):
            S.dma("sp", TRI[:, d, :], I["tri"][d], w=[rTRI])
            S.dma("sp", MSK[:, d, :], I["mask"][d], w=[rMSK])
        HM, rHM = P.sb("HM", [128, 4], F32)
        BLK, rBLK = P.sb("BLK", [128, 256], F32)
        S.dma("sp", HM[:], I["headmask"], w=[rHM])
        S.dma("sp", BLK[:], I["blkmask"], w=[rBLK])
        O64f, rO64f = P.sb("O64f", [128, 128], F32)
        O64, rO64 = P.sb("O64", [128, 128], BF16)
        S.dma("sp", O64f[:], I["ones64"], w=[rO64f])
        CP(K, O64[:], O64f[:], [rO64f], [rO64])
        gn, rgn = P.sb("gn", [128, DEPTH], F32)
        S.dma("sp", gn[:], I["gla_normT"], w=[rgn])

        GKVb = Rot(P.sbn("GKV", [128, 384], BF16, 2))
        GKTb = Rot(P.sbn("GKT", [128, 128], BF16, 2))
        GQTb = Rot(P.sbn("GQT", [128, 128], BF16, 2))
        LRb = Rot(P.sbn("LR", [128, 128], BF16, 2))
        for (t_, r_) in LRb.items:
            MSET(K, t_[:], 0.0, [r_])
        OGb = Rot(P.sbn("OG", [128, 2, 128], BF16, 2))
        OFb = Rot(P.sbn("OFl", [128, 2, 128], F32, 2))
        XB, rXB = P.sb("XB", [128, 128], F32)
        E1, rE1 = P.sb("E1", [128, 128], F32)
        LL, rLL = P.sb("LL", [128, 128], F32)
        EBT, rEBT = P.sb("EBT", [128, 128], F32)
        ENBT, rENBT = P.sb("ENBT", [128, 128], F32)
        ENB, rENB = P.sb("ENB", [128, 128], F32)
        QBT, rQBT = P.sb("QBT", [128, 128], BF16)
        KBT, rKBT = P.sb("KBT", [128, 128], BF16)
        KB, rKB = P.sb("KB", [128, 128], BF16)
        QBX, rQBX = P.sb("QBX", [128, 4, 128], BF16)
        ATM, rATM = P.sb("ATM", [128, 512], BF16)
        ST, rST = P.sb("ST", [128, 256], F32)
        STb, rSTb = P.sb("STb", [128, 256], BF16)
        DUM, rDUM = P.sb("DUM", [128, 256], F32)
        OS, rOS = P.sb("OS", [128, 2, 128], F32)
        SQ, rSQ = P.sb("SQ", [128, 2, 128], BF16)
        RS, rRS = P.sb("RS", [128, 2, 128], F32)
        SG, rSG = P.sb("SG", [128, 2, 128], F32)
        Yb = Rot(P.sbn("Y", [128, 2, 128], BF16, 2))
        psr = Rot(list(zip(K.PS, K.rPS)))

        utiles = [i * 128 for i in range(NT // 128)]
        ctx_t = utiles[:LC // 128]
        lat_t = utiles[LC // 128:]
        for b in range(NB):
            for d in range(2):
                order = (ctx_t + lat_t) if d == 0 else (ctx_t[::-1] + lat_t[::-1])
                if d == 1:
                    S.barrier()
                MSET(K, ST[:], 0.0, [rST])
                MSET(K, STb[:], 0.0, [rSTb])
                for u0 in order[:K.gla_nt]:
                    GKV, rGKV = GKVb.next()
                    GKT, rGKT = GKTb.next()
                    GQT, rGQT = GQTb.next()
                    LR, rLR = LRb.next()
                    S.dma("sp", GKV[:], SC["GKV"][b, u0:u0 + 128, :], w=[rGKV])
                    S.dma("sp", GKT[:], SC["GKT"][b, :, u0:u0 + 128], w=[rGKT])
                    S.dma("sp", GQT[:], SC["GQT"][b, :, u0:u0 + 128], w=[rGQT])
                    S.dma("sp", LR[0:16, :], SC["LRT"][b, d, :, u0:u0 + 128], w=[rLR])
                    if d == 1:
                        OG, rOG = OGb.next()
                        OFl, rOFl = OFb.next()
                        S.dma("sp", OG[:], SC["OGT"][b].rearrange("(c p) t -> p c t", p=128)[:, :, u0:u0 + 128], w=[rOG])
                        S.dma("sp", OFl[:], SC["OF"][b].rearrange("(c p) t -> p c t", p=128)[:, :, u0:u0 + 128], w=[rOFl])
                    ps, rps = psr.next()
                    MM(K, ps[:, 0:128], LR[:, :], Wgk[:, d, :], True, True, [rLR, rWgk], [rps])
                    TT(K, XB[:], ps[:, 0:128], BG[:, d, :], ALU.add, [rps, rBG], [rXB])
                    ACT(K, E1[:], XB[:], AF.Exp, [rXB], [rE1], scale=-1.0)
                    ACT(K, LL[:], E1[:], AF.Ln, [rE1], [rLL], bias=K.one_col[:, 0:1])
                    if K.gla_lvl < 2:
                        continue
                    psb, rpsb = psr.next()
                    MM(K, psb[:, 0:128], LL[:], TRI[:, d, :], True, True, [rLL, rTRI], [rpsb])
                    MM(K, psb[:, 128:256], TRI[:, d, :], LL[:], True, True, [rLL, rTRI], [rpsb])
                    ACT(K, EBT[:], psb[:, 0:128], AF.Exp, [rpsb], [rEBT])
                    ACT(K, ENBT[:], psb[:, 0:128], AF.Exp, [rpsb], [rENBT], scale=-1.0)
                    ACT(K, ENB[:], psb[:, 128:256], AF.Exp, [rpsb], [rENB], scale=-1.0)
                    if K.gla_lvl < 3:
                        continue
                    STT(K, QBT[:], GQT[:], qs, EBT[:], ALU.mult, ALU.mult, [rGQT, rEBT], [rQBT])
                    TT(K, KBT[:], GKT[:], ENBT[:], ALU.mult, [rGKT, rENBT], [rKBT])
                    TT(K, KB[:], GKV[:, 0:128], ENB[:], ALU.mult, [rGKV, rENB], [rKB])
                    if K.gla_x == 6:
                        for h in range(4):
                            ACT(K, QBX[:, h, :], QBT[:], AF.Identity, [rQBT, rHM], [rQBX], scale=HM[:, h:h + 1])
                    else:
                        TT(K, QBX[:], QBT[:].unsqueeze(1).broadcast_to([128, 4, 128]),
                           HM[:].unsqueeze(2).broadcast_to([128, 4, 128]), ALU.mult, [rQBT, rHM], [rQBX])
                    if K.gla_x == 5:
                        psr.next()
                    if K.gla_lvl < 4:
                        continue
                    psa, rpsa = psr.next()
                    if K.gla_x not in (2, 3, 4, 7):
                        MM(K, psa[:, 0:512], KBT[:], QBX[:].rearrange("p h i -> p (h i)"), True, True, [rKBT, rQBX], [rpsa])
                    if K.gla_x == 7:
                        TT(K, E1[:], ENB[:], ENB[:], ALU.mult, [rENB], [rE1])
                    elif K.gla_x == 3:
                        TT(K, ATM[:], MSK[:, d, :], MSK[:, d, :], ALU.mult, [rMSK], [rATM])
                    elif K.gla_x == 4:
                        TT(K, ATM[:, 0:128], psa[:, 0:128], ENB[:], ALU.mult, [rpsa, rENB], [rATM])
                    elif K.gla_x != 1:
                        TT(K, ATM[:], psa[:, 0:512], MSK[:, d, :], ALU.mult, [rpsa, rMSK], [rATM])
                    if K.gla_lvl < 5:
                        continue
                    psus = [psr.next(), psr.next()]
                    for c in range(2):
                        MM(K, psus[c][0][:, 0:256], KB[c * 64:(c + 1) * 64, :], GKV[c * 64:(c + 1) * 64, 128:384],
                           True, True, [rKB, rGKV], [psus[c][1]])
                    if K.gla_lvl < 6:
                        continue
                    pso, rpso = psr.next()
                    corder = (0, 1) if d == 0 else (1, 0)
                    first = True
                    for c in corder:
                        for hp in range(2):
                            MM(K, pso[:, hp * 128 + c * 64:hp * 128 + (c + 1) * 64], STb[:, hp * 128:(hp + 1) * 128],
                               QBT[:, c * 64:(c + 1) * 64], first, False, [rSTb, rQBT], [rpso], skip=True)
                            first = False
                        dcol = (c * 64 + 63) if d == 0 else (c * 64)
                        STT(K, DUM[:], psus[c][0][:, 0:256], EBT[:, dcol:dcol + 1], BLK[:], ALU.mult, ALU.mult,
                            [psus[c][1], rEBT, rBLK], [rDUM])
                        STT(K, ST[:], ST[:], EBT[:, dcol:dcol + 1], DUM[:], ALU.mult, ALU.add, [rST, rEBT, rDUM], [rST])
                        CP(K, STb[:], ST[:], [rST], [rSTb], eng="act")
                    for h in range(4):
                        MM(K, pso[(h % 2) * 64:(h % 2) * 64 + 64, (h // 2) * 128:(h // 2) * 128 + 128],
                           GKV[:, 128 + h * 64:128 + (h + 1) * 64], ATM[:, h * 128:(h + 1) * 128], False, h == 3,
                           [rGKV, rATM], [rpso], skip=True)
                    if K.gla_lvl < 7:
                        continue
                    if d == 0:
                        CP(K, OS[:], pso[:, 0:256].rearrange("p (c i) -> p c i", c=2), [rpso], [rOS], eng="act")
                        S.dma("pool", SC["OF"][b].rearrange("(c p) t -> p c t", p=128)[:, :, u0:u0 + 128], OS[:], r=[rOS])
                    else:
                        TT(K, OS[:], pso[:, 0:256].rearrange("p (c i) -> p c i", c=2), OFl[:], ALU.add, [rpso, rOFl], [rOS])
                        ACT(K, SQ[:], OS[:], AF.Square, [rOS], [rSQ])
                        psn, rpsn = psr.next()
                        for c in range(2):
                            MM(K, psn[:, c * 128:(c + 1) * 128], O64[:], SQ[:, c, :], True, True, [rO64, rSQ], [rpsn], skip=True)
                        rstd_from_psum(K, RS[:], psn[:, 0:256].rearrange("p (c i) -> p c i", c=2), 64, [rpsn], [rRS])
                        ACT(K, SG[:], OG[:], AF.Silu, [rOG], [rSG])
                        TT(K, RS[:], RS[:], SG[:], ALU.mult, [rRS, rSG], [rRS])
                        Y, rY = Yb.next()
                        STT(K, Y[:], OS[:], gn[:, l:l + 1], RS[:], ALU.mult, ALU.mult, [rOS, rgn, rRS], [rY])
                        S.dma("pool", SC["YT"][b, 768:1024].rearrange("(c p) t -> p c t", p=128)[:, :, u0:u0 + 128], Y[:], r=[rY])


def gla_gen(K, P, l, banks, blist):
    nc, S, I, SC = K.nc, K.S, K.I, K.SC
    qs = float(32 ** -0.5)
    Wgk, rWgk = P.sb("Wgk", [128, 2, 128], BF16)
    MSET(K, Wgk[:], 0.0, [rWgk])
    for d in range(2):
        load_w(K, P, Wgk[:, d, :], rWgk, I["w_gk"][l, d], 16, 128, three_d=False)
    BG, rBG = P.sb("BG", [128, 2, 128], F32)
    for d in range(2):
        S.dma("sp", BG[:, d, :], I["b_gk"][l, d:d + 1, :].partition_broadcast(128), w=[rBG])
    TRI, rTRI = P.sb("TRI", [128, 2, 128], F32)
    MSK, rMSK = P.sb("MSK", [128, 2, 512], F32)
    for d in range(2):
        S.dma("sp", TRI[:, d, :], I["tri"][d], w=[rTRI])
        S.dma("sp", MSK[:, d, :], I["mask"][d], w=[rMSK])
    HM, rHM = P.sb("HM", [128, 4], F32)
    BLK, rBLK = P.sb("BLK", [128, 256], F32)
    S.dma("sp", HM[:], I["headmask"], w=[rHM])
    S.dma("sp", BLK[:], I["blkmask"], w=[rBLK])
    O64f, rO64f = P.sb("O64f", [128, 128], F32)
    O64, rO64 = P.sb("O64", [128, 128], BF16)
    S.dma("sp", O64f[:], I["ones64"], w=[rO64f])
    CP(K, O64[:], O64f[:], [rO64f], [rO64])
    gn, rgn = P.sb("gn", [128, DEPTH], F32)
    S.dma("sp", gn[:], I["gla_normT"], w=[rgn])

    GKVb = Rot(P.sbn("GKV", [128, 384], BF16, 3))
    GKTb = Rot(P.sbn("GKT", [128, 128], BF16, 3))
    GQTb = Rot(P.sbn("GQT", [128, 128], BF16, 3))
    LRb = Rot(P.sbn("LR", [128, 128], BF16, 3))
    for (t_, r_) in LRb.items:
        MSET(K, t_[:], 0.0, [r_])
    OGb = Rot(P.sbn("OG", [128, 2, 128], BF16, 3))
    OFb = Rot(P.sbn("OFl", [128, 2, 128], F32, 3))
    XB, rXB = P.sb("XB", [128, 128], F32)
    E1, rE1 = P.sb("E1", [128, 128], F32)
    LL, rLL = P.sb("LL", [128, 128], F32)
    EBT, rEBT = P.sb("EBT", [128, 128], F32)
    ENBT, rENBT = P.sb("ENBT", [128, 128], F32)
    ENB, rENB = P.sb("ENB", [128, 128], F32)
    QBT, rQBT = P.sb("QBT", [128, 128], BF16)
    KBT, rKBT = P.sb("KBT", [128, 128], BF16)
    KB, rKB = P.sb("KB", [128, 128], BF16)
    QBX, rQBX = P.sb("QBX", [128, 4, 128], BF16)
    ATM, rATM = P.sb("ATM", [128, 512], BF16)
    ST, rST = P.sb("ST", [128, 256], F32)
    STb, rSTb = P.sb("STb", [128, 256], BF16)
    DUM2, rDUM2 = P.sb("DUM2", [128, 2, 256], F32)
    OS, rOS = P.sb("OS", [128, 2, 128], F32)
    SQ, rSQ = P.sb("SQ", [128, 2, 128], BF16)
    RS, rRS = P.sb("RS", [128, 2, 128], F32)
    SG, rSG = P.sb("SG", [128, 2, 128], F32)
    Yb = Rot(P.sbn("Y", [128, 2, 128], BF16, 2))
    psr = Rot([(K.PS[i], K.rPS[i]) for i in banks])

    utiles = [i * 128 for i in range(NT // 128)]
    ctx_t = utiles[:LC // 128]
    lat_t = utiles[LC // 128:]
    for d in range(2):
        if d == 1:
            yield "BARRIER"
        for b in blist:
            order = (ctx_t + lat_t) if d == 0 else (ctx_t[::-1] + lat_t[::-1])
            MSET(K, ST[:], 0.0, [rST])
            MSET(K, STb[:], 0.0, [rSTb])
            def issue_loads(u0_):
                GKV_, rGKV_ = GKVb.next()
                GKT_, rGKT_ = GKTb.next()
                GQT_, rGQT_ = GQTb.next()
                LR_, rLR_ = LRb.next()
                S.dma("sp", GKV_[:], SC["GKV"][b, u0_:u0_ + 128, :], w=[rGKV_])
                S.dma("sp", GKT_[:], SC["GKT"][b, :, u0_:u0_ + 128], w=[rGKT_])
                S.dma("sp", GQT_[:], SC["GQT"][b, :, u0_:u0_ + 128], w=[rGQT_])
                S.dma("sp", LR_[0:16, :], SC["LRT"][b, d, :, u0_:u0_ + 128], w=[rLR_])
                og = None
                if d == 1:
                    OG_, rOG_ = OGb.next()
                    OFl_, rOFl_ = OFb.next()
                    S.dma("sp", OG_[:], SC["OGT"][b].rearrange("(c p) t -> p c t", p=128)[:, :, u0_:u0_ + 128], w=[rOG_])
                    S.dma("sp", OFl_[:], SC["OF"][b].rearrange("(c p) t -> p c t", p=128)[:, :, u0_:u0_ + 128], w=[rOFl_])
                    og = (OG_, rOG_, OFl_, rOFl_)
                return (GKV_, rGKV_, GKT_, rGKT_, GQT_, rGQT_, LR_, rLR_, og)

            pending = issue_loads(order[0])
            for ui, u0 in enumerate(order):
                (GKV, rGKV, GKT, rGKT, GQT, rGQT, LR, rLR, og) = pending
                if og is not None:
                    (OG, rOG, OFl, rOFl) = og
                if ui + 1 < len(order):
                    pending = issue_loads(order[ui + 1])
                ps, rps = psr.next()
                MM(K, ps[:, 0:128], LR[:, :], Wgk[:, d, :], True, True, [rLR, rWgk], [rps])
                TT(K, XB[:], ps[:, 0:128], BG[:, d, :], ALU.add, [rps, rBG], [rXB])
                ACT(K, E1[:], XB[:], AF.Exp, [rXB], [rE1], scale=-1.0)
                ACT(K, LL[:], E1[:], AF.Ln, [rE1], [rLL], bias=K.one_col[:, 0:1])
                yield None
                psb, rpsb = psr.next()
                MM(K, psb[:, 0:128], LL[:], TRI[:, d, :], True, True, [rLL, rTRI], [rpsb])
                MM(K, psb[:, 128:256], TRI[:, d, :], LL[:], True, True, [rLL, rTRI], [rpsb])
                ACT(K, EBT[:], psb[:, 0:128], AF.Exp, [rpsb], [rEBT])
                ACT(K, ENBT[:], psb[:, 0:128], AF.Exp, [rpsb], [rENBT], scale=-1.0)
                ACT(K, ENB[:], psb[:, 128:256], AF.Exp, [rpsb], [rENB], scale=-1.0)
                STT(K, QBT[:], GQT[:], qs, EBT[:], ALU.mult, ALU.mult, [rGQT, rEBT], [rQBT])
                TT(K, KBT[:], GKT[:], ENBT[:], ALU.mult, [rGKT, rENBT], [rKBT])
                TT(K, KB[:], GKV[:, 0:128], ENB[:], ALU.mult, [rGKV, rENB], [rKB])
                TT(K, QBX[:], QBT[:].unsqueeze(1).broadcast_to([128, 4, 128]),
                   HM[:].unsqueeze(2).broadcast_to([128, 4, 128]), ALU.mult, [rQBT, rHM], [rQBX])
                yield None
                psa, rpsa = psr.next()
                MM(K, psa[:, 0:512], KBT[:], QBX[:].rearrange("p h i -> p (h i)"), True, True, [rKBT, rQBX], [rpsa])
                TT(K, ATM[:], psa[:, 0:512], MSK[:, d, :], ALU.mult, [rpsa, rMSK], [rATM])
                psus = [psr.next(), psr.next()]
                for c in range(2):
                    MM(K, psus[c][0][:, 0:256], KB[c * 64:(c + 1) * 64, :], GKV[c * 64:(c + 1) * 64, 128:384],
                       True, True, [rKB, rGKV], [psus[c][1]])
                for c in range(2):
                    dcol = (c * 64 + 63) if d == 0 else (c * 64)
                    STT(K, DUM2[:, c, :], psus[c][0][:, 0:256], EBT[:, dcol:dcol + 1], BLK[:], ALU.mult, ALU.mult,
                        [psus[c][1], rEBT, rBLK], [rDUM2])
                yield None
                pso, rpso = psr.next()
                corder = (0, 1) if d == 0 else (1, 0)
                first = True
                for c in corder:
                    for hp in range(2):
                        MM(K, pso[:, hp * 128 + c * 64:hp * 128 + (c + 1) * 64], STb[:, hp * 128:(hp + 1) * 128],
                           QBT[:, c * 64:(c + 1) * 64], first, False, [rSTb, rQBT], [rpso], skip=True)
                        first = False
                    dcol = (c * 64 + 63) if d == 0 else (c * 64)
                    STT(K, ST[:], ST[:], EBT[:, dcol:dcol + 1], DUM2[:, c, :], ALU.mult, ALU.add, [rST, rEBT, rDUM2], [rST])
                    CP(K, STb[:], ST[:], [rST], [rSTb], eng="act")
                yield None
                for h in range(4):
                    MM(K, pso[(h % 2) * 64:(h % 2) * 64 + 64, (h // 2) * 128:(h // 2) * 128 + 128],
                       GKV[:, 128 + h * 64:128 + (h + 1) * 64], ATM[:, h * 128:(h + 1) * 128], False, h == 3,
                       [rGKV, rATM], [rpso], skip=True)
                yield None
                if d == 0:
                    CP(K, OS[:], pso[:, 0:256].rearrange("p (c i) -> p c i", c=2), [rpso], [rOS], eng="act")
                    S.dma("pool", SC["OF"][b].rearrange("(c p) t -> p c t", p=128)[:, :, u0:u0 + 128], OS[:], r=[rOS])
                else:
                    TT(K, OS[:], pso[:, 0:256].rearrange("p (c i) -> p c i", c=2), OFl[:], ALU.add, [rpso, rOFl], [rOS])
                    ACT(K, SQ[:], OS[:], AF.Square, [rOS], [rSQ])
                    psn, rpsn = psr.next()
                    for c in range(2):
                        MM(K, psn[:, c * 128:(c + 1) * 128], O64[:], SQ[:, c, :], True, True, [rO64, rSQ], [rpsn], skip=True)
                    rstd_from_psum(K, RS[:], psn[:, 0:256].rearrange("p (c i) -> p c i", c=2), 64, [rpsn], [rRS])
                    ACT(K, SG[:], OG[:], AF.Silu, [rOG], [rSG])
                    TT(K, RS[:], RS[:], SG[:], ALU.mult, [rRS, rSG], [rRS])
                    Y, rY = Yb.next()
                    STT(K, Y[:], OS[:], gn[:, l:l + 1], RS[:], ALU.mult, ALU.mult, [rOS, rgn, rRS], [rY])
                    S.dma("pool", SC["YT"][b, 768:1024].rearrange("(c p) t -> p c t", p=128)[:, :, u0:u0 + 128], Y[:], r=[rY])


def mla_gen(K, P, l, obanks, sbanks):
    nc, S, I, SC = K.nc, K.S, K.I, K.SC
    scale = float(96 ** -0.5)
    NKC = NT // 128
    VB, rVB = P.sb("VB", [128, NKC, 8, 128], BF16)
    KTb = Rot(P.sbn("KT", [96, NT], BF16, 2))
    QTb = Rot(P.sbn("QT", [96, 512], BF16, 3))
    PTb = Rot(P.sbn("PT", [128, 512], BF16, 6))
    R_, rR = P.sb("R", [64, 512], F32)
    Yb = Rot(P.sbn("Y", [64, 512], BF16, 2))
    obank = Rot([(K.PS[i], K.rPS[i]) for i in obanks])
    sbank = Rot([(K.PS[i], K.rPS[i]) for i in sbanks])
    units = [(b, h, ti, t) for b in range(NB) for h in range(8) for ti, t in enumerate(tiles_of_batch())]
    kts = {}

    def load_unit(i):
        b, h, ti, (u0, T, isc, pos0) = units[i]
        if ti == 0:
            KT_, rKT_ = KTb.next()
            S.dma("sp", KT_[:, :], SC["KT"][b, h], w=[rKT_])
            kts[(b, h)] = (KT_, rKT_)
        QT_, rQT_ = QTb.next()
        S.dma("sp", QT_[:, 0:T], SC["QT"][b, h, :, u0:u0 + T], w=[rQT_])
        return (QT_, rQT_)

    pending = load_unit(0)
    for ui, (b, h, ti, (u0, T, isc, pos0)) in enumerate(units):
        if h == 0 and ti == 0:
            vsrc = SC["V"][b].rearrange("(n p) h c -> p n (h c)", p=128)
            vdst = VB[:].rearrange("p n h c -> p n (h c)")
            for n0 in range(0, NKC, 6):
                n1 = min(NKC, n0 + 6)
                S.dma("sp", vdst[:, n0:n1], vsrc[:, n0:n1], w=[rVB])
        QT, rQT = pending
        KT, rKT = kts[(b, h)]
        if ui + 1 < len(units):
            pending = load_unit(ui + 1)
        nk = (LC // 128) if isc else NKC
        O, rO = obank.next()
        sps = {}

        def smm(kc, KT=KT, rKT=rKT, QT=QT, rQT=rQT, T=T, sps=sps):
            ps, rps = sbank.next()
            MM(K, ps[:, 0:T], KT[:, kc * 128:(kc + 1) * 128], QT[:, 0:T], True, True, [rKT, rQT], [rps])
            sps[kc] = (ps, rps)

        for k0 in range(min(3, nk)):
            smm(k0)
        for kc in range(nk):
            if kc + 3 < nk:
                smm(kc + 3)
            ps, rps = sps.pop(kc)
            PT, rPT = PTb.next()
            ACT(K, PT[:, 0:T], ps[:, 0:T], AF.Exp, [rps], [rPT], scale=scale)
            MM(K, O[:, 0:T], VB[:, kc, h, :], PT[:, 0:T], kc == 0, kc == nk - 1, [rVB, rPT], [rO])
            yield None
        CP(K, R_[0:64, 0:T], O[64:128, 0:T], [rO], [rR], eng="act")
        RECIP(K, R_[0:64, 0:T], R_[0:64, 0:T], [rR], [rR])
        Y, rY = Yb.next()
        TT(K, Y[0:64, 0:T], O[0:64, 0:T], R_[0:64, 0:T], ALU.mult, [rO, rR], [rY])
        S.dma("pool", SC["YT"][b, 256 + h * 64:256 + (h + 1) * 64, u0:u0 + T], Y[0:64, 0:T], r=[rY])


def phase_mix(K, l):
    with Phase(K, "X%d" % l) as P:
        gm = mla_gen(K, P, l, (0, 1), (2, 3, 4, 5, 6, 7))
        ggs = [gla_gen(K, P, l, (4, 5), [0]), gla_gen(K, P, l, (6, 7), [1])]
        alive = [True, True]
        waiting = [False, False]
        import os
        skip = os.environ.get("MIX_SKIP", "")
        if os.environ.get("MIX_MODE", "seq") == "seq":
            for _ in gm:
                pass
        if skip == "gla":
            alive = [False, False]
        n_mla = NB * 8 * ((L // 512) * (NT // 128) + LC // 128)
        n_gla = 2 * NB * (NT // 128) * 5
        ratio = n_mla / float(n_gla)
        acc = 0.0
        mla_done = (skip == "mla")
        turn = 0
        while any(alive) or not mla_done:
            if any(alive):
                if all(waiting[i] or not alive[i] for i in range(2)):
                    K.S.barrier()
                    waiting = [False, False]
                i = turn % 2
                turn += 1
                if not alive[i] or waiting[i]:
                    i = 1 - i
                if alive[i] and not waiting[i]:
                    try:
                        tok = next(ggs[i])
                        if tok == "BARRIER":
                            waiting[i] = True
                    except StopIteration:
                        alive[i] = False
            acc += ratio
            while (acc >= 1.0 or not any(alive)) and not mla_done:
                acc -= 1.0
                try:
                    next(gm)
                except StopIteration:
                    mla_done = True
                    break


def phase_D1(K, l):
    nc, S, I, SC = K.nc, K.S, K.I, K.SC
    with Phase(K, "D%d" % l) as P:
        Wo, rWo = P.sb("Wo", [128, 8, 1024], BF16)
        load_w(K, P, Wo, rWo, I["w_o"][l], 1024, 1024)
        Xb = Rot(P.sbn("X", [128, 8, 512], F32, 2))
        Yb = Rot(P.sbn("Y", [128, 8, 512], BF16, 2))
        SQ, rSQ = P.sb("SQ", [128, 8, 512], BF16)
        RS, rRS = P.sb("RS", [128, 512], F32)
        TMP, rTMP = P.sb("TMP", [128, 512], F32)
        Hb = Rot(P.sbn("H", [128, 8, 512], BF16, 2))
        psr = Rot(list(zip(K.PS, K.rPS)))
        src_x = K.x_src(l)
        for b in range(NB):
            xv = src_x[b].rearrange("(k p) t -> p k t", p=128)
            xo = SC["XT"][b].rearrange("(k p) t -> p k t", p=128)
            yv = SC["YT"][b].rearrange("(k p) t -> p k t", p=128)
            hv = SC["H2T"][b].rearrange("(k p) t -> p k t", p=128)
            for (u0, T, isc, pos0) in tiles_of_batch():
                if isc and l == DEPTH - 1:
                    continue
                j = 2 if isc else b
                X, rX = Xb.next()
                Y, rY = Yb.next()
                S.dma("sp", X[:, :, 0:T], xv[:, :, u0:u0 + T], w=[rX])
                S.dma("sp", Y[:, :, 0:T], yv[:, :, u0:u0 + T], w=[rY])
                for fc in range(8):
                    ps, rps = psr.next()
                    for k in range(8):
                        MM(K, ps[:, 0:T], Wo[:, k, fc * 128:(fc + 1) * 128], Y[:, k, 0:T], k == 0, k == 7, [rWo, rY], [rps])
                    STT(K, X[:, fc, 0:T], ps[:, 0:T], K.MOD[:, l, 2, fc, j:j + 1], X[:, fc, 0:T], ALU.mult, ALU.add,
                        [rps, K.rMOD, rX], [rX])
                S.dma("pool", xo[:, :, u0:u0 + T], X[:, :, 0:T], r=[rX])
                ACT(K, SQ[:, :, 0:T], X[:, :, 0:T], AF.Square, [rX], [rSQ])
                ps, rps = psr.next()
                for k in range(8):
                    MM(K, ps[:, 0:T], K.ONESB[:, :], SQ[:, k, 0:T], k == 0, k == 7, [rSQ, K.rONESB], [rps])
                rstd_from_psum(K, RS[:, 0:T], ps[:, 0:T], D, [rps], [rRS])
                H, rH = Hb.next()
                for k in range(8):
                    STT(K, TMP[:, 0:T], X[:, k, 0:T], K.MOD[:, l, 4, k, j:j + 1], RS[:, 0:T], ALU.mult, ALU.mult,
                        [rX, rRS, K.rMOD], [rTMP])
                    ACT(K, H[:, k, 0:T], TMP[:, 0:T], AF.Identity, [rTMP, K.rMOD], [rH], bias=K.MOD[:, l, 3, k, j:j + 1])
                S.dma("pool", hv[:, :, u0:u0 + T], H[:, :, 0:T], r=[rH])


def phase_D2(K, l):
    nc, S, I, SC = K.nc, K.S, K.I, K.SC
    T = 256
    with Phase(K, "F%d" % l) as P:
        Wup, rWup = P.sb("Wup", [128, 8, 2 * DFF], BF16)
        Wdn, rWdn = P.sb("Wdn", [128, NFF, 1024], BF16)
        load_w(K, P, Wup, rWup, I["w_up"][l], 1024, 2 * DFF)
        load_w(K, P, Wdn, rWdn, I["w_down"][l], DFF, 1024)
        cw, rcw = P.sb("cw", [128, DEPTH, 3, NFF], F32)
        cb, rcb = P.sb("cb", [128, DEPTH, NFF], F32)
        S.dma("sp", cw[:], I["conv_wT"], w=[rcw])
        S.dma("sp", cb[:], I["conv_bT"], w=[rcb])
        Xb = Rot(P.sbn("X", [128, 8, T], F32, 2))
        Hb = Rot(P.sbn("H", [128, 8, T + 2], BF16, 2))
        C1, rC1 = P.sb("C1", [128, T], F32)
        C2, rC2 = P.sb("C2", [128, T], F32)
        C3, rC3 = P.sb("C3", [128, T], F32)
        SL, rSL = P.sb("SL", [128, T], F32)
        A_, rA = P.sb("ACTV", [128, NFF, T], BF16)
        psr = Rot(list(zip(K.PS, K.rPS)))
        for b in range(NB):
            xo = SC["XT"][b].rearrange("(k p) t -> p k t", p=128)
            hv = SC["H2T"][b].rearrange("(k p) t -> p k t", p=128)
            for (s0, slen) in ((0, LC), (LC, L)):
                if s0 == 0 and l == DEPTH - 1:
                    continue
                j = 2 if s0 == 0 else b
                for u0 in range(s0, s0 + slen, T):
                    X, rX = Xb.next()
                    H, rH = Hb.next()
                    lo = max(u0 - 1, s0)
                    hi = min(u0 + T + 1, s0 + slen)
                    if lo > u0 - 1:
                        MSET(K, H[:, :, 0:1], 0.0, [rH])
                    if hi < u0 + T + 1:
                        MSET(K, H[:, :, T + 1:T + 2], 0.0, [rH])
                    S.dma("sp", H[:, :, lo - (u0 - 1):hi - (u0 - 1)], hv[:, :, lo:hi], w=[rH])
                    S.dma("sp", X[:, :, :], xo[:, :, u0:u0 + T], w=[rX])
                    for cf in range(NFF):
                        psu, rpsu = psr.next()
                        for k in range(8):
                            MM(K, psu[:, 0:T], Wup[:, k, cf * 128:(cf + 1) * 128], H[:, k, 1:T + 1], k == 0, k == 7,
                               [rWup, rH], [rpsu])
                        psg, rpsg = psr.next()
                        for k in range(8):
                            MM(K, psg[:, 0:T + 2], Wup[:, k, DFF + cf * 128:DFF + (cf + 1) * 128], H[:, k, 0:T + 2], k == 0, k == 7,
                               [rWup, rH], [rpsg])
                        TS(K, C1[:], psg[:, 0:T], cw[:, l, 0, cf:cf + 1], cb[:, l, cf:cf + 1], ALU.mult, ALU.add,
                           [rpsg, rcw, rcb], [rC1])
                        STT(K, C2[:], psg[:, 1:T + 1], cw[:, l, 1, cf:cf + 1], C1[:], ALU.mult, ALU.add, [rpsg, rcw, rC1], [rC2])
                        STT(K, C3[:], psg[:, 2:T + 2], cw[:, l, 2, cf:cf + 1], C2[:], ALU.mult, ALU.add, [rpsg, rcw, rC2], [rC3])
                        ACT(K, SL[:], C3[:], AF.Silu, [rC3], [rSL])
                        TT(K, A_[:, cf, :], psu[:, 0:T], SL[:], ALU.mult, [rpsu, rSL], [rA])
                    for fc in range(8):
                        ps, rps = psr.next()
                        for cf in range(NFF):
                            MM(K, ps[:, 0:T], Wdn[:, cf, fc * 128:(fc + 1) * 128], A_[:, cf, :], cf == 0, cf == NFF - 1,
                               [rWdn, rA], [rps])
                        STT(K, X[:, fc, :], ps[:, 0:T], K.MOD[:, l, 5, fc, j:j + 1], X[:, fc, :], ALU.mult, ALU.add,
                            [rps, K.rMOD, rX], [rX])
                    S.dma("pool", xo[:, :, u0:u0 + T], X[:, :, :], r=[rX])


def phase_final(K):
    nc, S, I, SC = K.nc, K.S, K.I, K.SC
    T = 512
    with Phase(K, "Z") as P:
        nf, rnf = P.sb("nf", [128, 8], F32)
        S.dma("sp", nf[:], I["norm_fT"], w=[rnf])
        Xb = Rot(P.sbn("X", [128, 8, T], F32, 2))
        Ob = Rot(P.sbn("O", [128, 8, T], F32, 2))
        SQ, rSQ = P.sb("SQ", [128, 8, T], BF16)
        RS, rRS = P.sb("RS", [128, T], F32)
        psr = Rot(list(zip(K.PS, K.rPS)))
        src_x = K.x_src(DEPTH)
        for b in range(NB):
            xv = src_x[b].rearrange("(k p) t -> p k t", p=128)
            ov = K.OUT[b].rearrange("(k p) t -> p k t", p=128)
            for i in range(L // T):
                u0 = LC + i * T
                X, rX = Xb.next()
                O, rO = Ob.next()
                S.dma("sp", X[:], xv[:, :, u0:u0 + T], w=[rX])
                ACT(K, SQ[:], X[:], AF.Square, [rX], [rSQ])
                ps, rps = psr.next()
                for k in range(8):
                    MM(K, ps[:, 0:T], K.ONESB[:, :], SQ[:, k, :], k == 0, k == 7, [rSQ, K.rONESB], [rps])
                rstd_from_psum(K, RS[:], ps[:, 0:T], D, [rps], [rRS])
                for k in range(8):
                    STT(K, O[:, k, :], X[:, k, :], nf[:, k:k + 1], RS[:], ALU.mult, ALU.mult, [rX, rnf, rRS], [rO])
                S.dma("pool", ov[:, :, i * T:(i + 1) * T], O[:], r=[rO])


IN_SPECS = {
    "xT": ([NB, D, NT], F32), "cT": ([128, 8, 3], F32), "w_ada": ([DEPTH, D, 6 * D], F32),
    "b_adaT": ([128, DEPTH * 48], F32), "norm1T": ([128, DEPTH * 8], F32), "norm2T": ([128, DEPTH * 8], F32),
    "norm_fT": ([128, 8], F32), "w_in_ext": ([DEPTH, D, INW], F32), "q_normT": ([128, DEPTH * 2], F32),
    "kv_normT": ([128, DEPTH], F32), "w_uq_ext": ([DEPTH, 256, 1024], F32), "w_ukv_r": ([DEPTH, 128, 1024], F32),
    "w_gk": ([DEPTH, 2, 16, 128], F32), "b_gk": ([DEPTH, 2, 128], F32), "gla_normT": ([128, DEPTH], F32),
    "w_pool_bd": ([DEPTH, 2, 128, 128], F32), "pool_scaleT": ([128, DEPTH * 2], F32),
    "w_o": ([DEPTH, D, D], F32), "w_up": ([DEPTH, D, 2 * DFF], F32), "w_down": ([DEPTH, DFF, D], F32),
    "conv_wT": ([128, DEPTH, 3, NFF], F32), "conv_bT": ([128, DEPTH, NFF], F32),
    "cos_t": ([128, L], F32), "sin_t": ([128, L], F32), "tri": ([2, 128, 128], F32), "mask": ([2, 128, 512], F32),
    "headmask": ([128, 4], F32), "blkmask": ([128, 256], F32), "ones64": ([128, 128], F32),
    "invcnt": ([256, NT], F32),
}

SCRATCH = {
    "XT": ([NB, D, NT], F32), "KT": ([NB, 8, 96, NT], BF16), "V": ([NB, NT, 8, 128], BF16),
    "QT": ([NB, 8, 96, NT], BF16), "POOLT": ([NB, 256, NT], F32), "GKT": ([NB, 128, NT], BF16),
    "GKV": ([NB, NT, 384], BF16), "GQT": ([NB, 128, NT], BF16), "LRT": ([NB, 2, 16, NT], BF16),
    "OGT": ([NB, 256, NT], BF16), "YT": ([NB, D, NT], BF16), "H2T": ([NB, D, NT], BF16), "OF": ([NB, 256, NT], F32),
}


def build(n_layers=DEPTH, phases="APXDF", debug=()):
    nc = bass.Bass("TRN2", target_bir_lowering=False)
    K = K_()
    K.nc = nc
    K.I = {k: nc.dram_tensor(k, shp, dt, kind="ExternalInput").ap() for k, (shp, dt) in IN_SPECS.items()}
    K.SC = {}
    for k, (shp, dt) in SCRATCH.items():
        kind = "ExternalOutput" if k in debug else "Internal"
        K.SC[k] = nc.dram_tensor("sc_" + k, shp, dt, kind=kind).ap()
    K.OUT = nc.dram_tensor("outT", [NB, D, L], F32, kind="ExternalOutput").ap()
    K.S = S = Sched(nc)
    S.bar_src = K.I["tri"][0, 0:1, 0:64]
    K.x_src = lambda l: (K.I["xT"] if l == 0 else K.SC["XT"])
    import os
    K.gla_lvl = int(os.environ.get("GLA_LVL", "9"))
    K.gla_nt = int(os.environ.get("GLA_NT", "999"))
    K.gla_x = int(os.environ.get("GLA_X", "0"))
    with ExitStack() as st:
        K.PS, K.rPS = [], []
        for i in range(8):
            K.PS.append(st.enter_context(nc.psum_tensor("psb%d" % i, [128, 512], F32)))
            K.rPS.append(S.res("@ps%d" % i))
        K.MOD = st.enter_context(nc.sbuf_tensor("MOD", [128, DEPTH, 6, 8, 3], F32))
        K.rMOD = S.res("@MOD")
        K.ONESB = st.enter_context(nc.sbuf_tensor("ONESB", [128, 128], BF16))
        K.rONESB = S.res("@ONESB")
        K.eps_col = st.enter_context(nc.sbuf_tensor("epsc", [128, 1], F32))
        K.one_col = st.enter_context(nc.sbuf_tensor("onec", [128, 1], F32))
        rc = S.res("@cols")
        MSET(K, K.ONESB[:], 1.0, [K.rONESB])
        MSET(K, K.eps_col[:], EPS, [rc])
        MSET(K, K.one_col[:], 1.0, [rc])
        if "debug_mod" in debug:
            K.MODO = nc.dram_tensor("modo", [128, DEPTH * 6 * 8 * 3], F32, kind="ExternalOutput").ap()
        phase_mod(K)
        if "debug_mod" in debug:
            S.dma("sp", K.MODO, K.MOD[:].rearrange("p a b c d -> p (a b c d)"), r=[K.rMOD])
            S.barrier()
        for l in range(n_layers):
            if "A" in phases:
                phase_A(K, l)
            if "P" in phases:
                phase_pool(K, l)
            if "X" in phases:
                phase_mix(K, l)
            if "M" in phases:
                phase_mla(K, l)
            if "G" in phases:
                phase_gla(K, l)
            if "D" in phases:
                phase_D1(K, l)
            if "F" in phases:
                phase_D2(K, l)
        if n_layers == DEPTH:
            phase_final(K)
        K.stats = S.emit()
    return nc, K


def _colT(v, k):
    return np.ascontiguousarray(np.asarray(v, np.float32).reshape(k, 128).T)


def shared_inputs(inp):
    f = lambda a: np.ascontiguousarray(np.asarray(a, np.float32))
    o = {}
    o["w_ada"] = f(inp["w_ada"])
    o["b_adaT"] = np.concatenate([_colT(inp["b_ada"][l], 48) for l in range(DEPTH)], axis=1)
    o["norm1T"] = np.concatenate([_colT(inp["norm1"][l], 8) for l in range(DEPTH)], axis=1)
    o["norm2T"] = np.concatenate([_colT(inp["norm2"][l], 8) for l in range(DEPTH)], axis=1)
    o["norm_fT"] = _colT(inp["norm_f"], 8)
    perm = np.zeros(32, np.int64)
    sign = np.zeros(32, np.float32)
    for ax in range(2):
        for hf in range(2):
            for fq in range(8):
                i = ax * 16 + hf * 8 + fq
                perm[i] = ax * 16 + (1 - hf) * 8 + fq
                sign[i] = -1.0 if hf == 0 else 1.0
    w_in = f(inp["w_in"])
    kr = w_in[:, :, 128:160]
    o["w_in_ext"] = np.ascontiguousarray(np.concatenate([w_in, kr[:, :, perm]], axis=2))
    o["q_normT"] = np.concatenate([_colT(inp["q_norm"][l], 2) for l in range(DEPTH)], axis=1)
    o["kv_normT"] = np.concatenate([_colT(inp["kv_norm"][l], 1) for l in range(DEPTH)], axis=1)
    wuq = f(inp["w_uq"]).reshape(DEPTH, 256, 8, 96)
    nope = wuq[:, :, :, :64].reshape(DEPTH, 256, 512)
    rope = wuq[:, :, :, 64:]
    ropep = rope[:, :, :, perm]
    o["w_uq_ext"] = np.ascontiguousarray(np.concatenate([nope, rope.reshape(DEPTH, 256, 256), ropep.reshape(DEPTH, 256, 256)], axis=2))
    wukv = f(inp["w_ukv"]).reshape(DEPTH, 128, 8, 128)
    o["w_ukv_r"] = np.ascontiguousarray(np.concatenate([wukv[:, :, :, :64].reshape(DEPTH, 128, 512),
                                                        wukv[:, :, :, 64:].reshape(DEPTH, 128, 512)], axis=2))
    o["w_gk"] = np.ascontiguousarray(np.stack([f(inp["w_gk_f"]), f(inp["w_gk_b"])], axis=1))
    o["b_gk"] = np.ascontiguousarray(np.stack([f(inp["b_gk_f"]), f(inp["b_gk_b"])], axis=1))
    gn = f(inp["gla_norm"])
    o["gla_normT"] = np.ascontiguousarray(np.concatenate([gn, gn], axis=1).T)
    wp = f(inp["w_pool"])
    bd = np.zeros((DEPTH, 2, 128, 128), np.float32)
    for g in range(4):
        c, s = g // 2, (g % 2) * 64
        bd[:, c, s:s + 64, s:s + 64] = wp[:, g]
    o["w_pool_bd"] = bd
    o["pool_scaleT"] = np.concatenate([_colT(inp["pool_scale"][l], 2) for l in range(DEPTH)], axis=1)
    o["w_o"] = f(inp["w_o"])
    o["w_up"] = f(inp["w_up"])
    o["w_down"] = f(inp["w_down"])
    cw = f(inp["conv_w"]).reshape(DEPTH, 3, NFF, 128)
    o["conv_wT"] = np.ascontiguousarray(cw.transpose(3, 0, 1, 2))
    o["conv_bT"] = np.ascontiguousarray(f(inp["conv_b"]).reshape(DEPTH, NFF, 128).transpose(2, 0, 1))
    rows = L // 64
    row = np.repeat(np.arange(rows), 64).astype(np.float32)
    col = np.tile(np.arange(64), rows).astype(np.float32)
    inv = (np.float32(10000.0) ** (-np.arange(0, 16, 2, dtype=np.float32) / np.float32(16))).astype(np.float32)
    ar = row[:, None] * inv
    ac = col[:, None] * inv
    ang = np.concatenate([ar, ar, ac, ac], axis=-1).astype(np.float32)
    cosT = np.cos(ang).astype(np.float32).T
    sinT = (np.sin(ang).astype(np.float32) * sign[None, :]).T
    o["cos_t"] = np.ascontiguousarray(np.tile(cosT, (4, 1)))
    o["sin_t"] = np.ascontiguousarray(np.tile(sinT, (4, 1)))
    jj = np.arange(128)[:, None]
    ii = np.arange(128)[None, :]
    same = (jj // 64) == (ii // 64)
    tri_f = (same & (jj <= ii)).astype(np.float32)
    tri_b = (same & (jj >= ii)).astype(np.float32)
    o["tri"] = np.stack([tri_f, tri_b]) * np.float32(-1.0 / 16.0)
    o["mask"] = np.stack([np.tile(tri_f, (1, 4)), np.tile(tri_b, (1, 4))]).astype(np.float32)
    hm = np.zeros((128, 4), np.float32)
    for h in range(4):
        hm[h * 32:(h + 1) * 32, h] = 1.0
    o["headmask"] = hm
    blk = np.zeros((128, 256), np.float32)
    for h in range(4):
        blk[h * 32:(h + 1) * 32, h * 64:(h + 1) * 64] = 1.0
    o["blkmask"] = blk
    o64 = np.zeros((128, 128), np.float32)
    o64[0:64, 0:64] = 1.0
    o64[64:128, 64:128] = 1.0
    o["ones64"] = o64
    ivc = np.zeros((256, NT), np.float32)
    for g, w in enumerate((2, 4, 8, 16)):
        for (s0, sl) in ((0, LC), (LC, L)):
            t = np.arange(sl)
            lo = np.clip(t - w // 2, 0, sl)
            hi = np.clip(t - w // 2 + w, 0, sl)
            ivc[g * 64:(g + 1) * 64, s0:s0 + sl] = (1.0 / (hi - lo).astype(np.float32))[None, :]
    o["invcnt"] = ivc
    return o


def core_inputs(inp, core, shared):
    b0 = core * NB
    x = np.asarray(inp["x"][b0:b0 + NB], np.float32)
    cx = np.asarray(inp["ctx"][b0:b0 + NB], np.float32)
    xT = np.ascontiguousarray(np.concatenate([cx, x], axis=1).transpose(0, 2, 1))
    vecs = np.stack([np.asarray(inp["c"][b0], np.float32), np.asarray(inp["c"][b0 + 1], np.float32),
                     np.asarray(inp["c_ctx"], np.float32)], axis=1)
    cT = np.ascontiguousarray(vecs.reshape(8, 128, 3).transpose(1, 0, 2))
    d = dict(shared)
    d["xT"] = xT
    d["cT"] = cT
    return d


_CACHE = {}


def kernel(**inputs):
    if "nc" not in _CACHE:
        _CACHE["nc"] = build()[0]
    nc = _CACHE["nc"]
    shared = shared_inputs(inputs)
    n = 8
    in_maps = [core_inputs(inputs, c, shared) for c in range(n)]
    res = run_bass_kernel_spmd(nc, in_maps, core_ids=list(range(n)))
    outs = [np.asarray(r["outT"]).transpose(0, 2, 1) for r in res.results]
    return np.ascontiguousarray(np.concatenate(outs, axis=0).astype(np.float32))
```
